# Optimizing a Trainium2 kernel written in Bass

```python
import math
import jax, jax.numpy as jnp
from jax import lax
import numpy as np

D_MODEL = 2048
BATCH = 1
SEQ = 8192
DEPTH = 2

HEAD_DIM = 64
D_PLE = 256
BLOCK = 128
WINDOW = 128
RMS_EPS = 1e-6
A_HEADS = 16
A_KV_HEADS = 2
A_GROUP = A_HEADS // A_KV_HEADS
A_WIDTH = A_HEADS * HEAD_DIM
A_KV_WIDTH = A_KV_HEADS * HEAD_DIM
B_HEADS = 8
B_VDIM = 2 * HEAD_DIM
B_WIDTH = B_HEADS * B_VDIM
C_HEADS = 32
C_WIDTH = C_HEADS * HEAD_DIM
N_EVEN = (DEPTH + 1) // 2
N_ODD = DEPTH // 2
AB_SPLITS = (A_WIDTH, A_KV_WIDTH, A_KV_WIDTH, A_WIDTH,
             B_HEADS * 2 * HEAD_DIM, B_HEADS * 2 * HEAD_DIM, B_WIDTH, B_WIDTH)
AB_IN = sum(AB_SPLITS)
AB_OUT = A_WIDTH + B_WIDTH
C_SPLITS = (C_WIDTH, C_WIDTH, C_WIDTH, C_HEADS, C_WIDTH)
C_IN = sum(C_SPLITS)

kernel_name = "hybrid_swa_sink_diff_fox_gated_ple"


def rmsnorm(x, g):
    xf = x.astype(jnp.float32)
    y = xf * lax.rsqrt(jnp.mean(xf * xf, axis=-1, keepdims=True) + RMS_EPS)
    return (y * g.astype(jnp.float32)).astype(x.dtype)


def split_cols(z, sizes):
    idx = np.cumsum(sizes)[:-1].tolist()
    return jnp.split(z, idx, axis=-1)


def alibi_slopes(n):
    return jnp.asarray([2.0 ** (-8.0 * (h + 1) / n) for h in range(n)], dtype=jnp.float32)


def sliding_window_sink_attn(q, k, v, sinks):
    b, s = q.shape[0], q.shape[1]
    nb = s // BLOCK
    qb = q.reshape(b, nb, BLOCK, A_KV_HEADS, A_GROUP, HEAD_DIM)

    def band(t):
        prev = jnp.pad(t, ((0, 0), (BLOCK, 0), (0, 0), (0, 0)))[:, :s]
        return jnp.concatenate([prev.reshape(b, nb, BLOCK, A_KV_HEADS, HEAD_DIM),
                                t.reshape(b, nb, BLOCK, A_KV_HEADS, HEAD_DIM)], axis=2)

    kb, vb = band(k), band(v)
    scores = jnp.einsum('bnikgd,bnjkd->bnkgij', qb, kb).astype(jnp.float32) * (HEAD_DIM ** -0.5)
    i = jnp.arange(BLOCK)[:, None]
    j = jnp.arange(2 * BLOCK)[None, :]
    dist = i - j + BLOCK
    key_pos = jnp.arange(nb)[:, None, None] * BLOCK - BLOCK + j[None]
    valid = (dist >= 0) & (dist < WINDOW) & (key_pos >= 0)
    slopes = alibi_slopes(A_HEADS).reshape(A_KV_HEADS, A_GROUP)[:, :, None, None]
    scores = scores - slopes * dist.astype(jnp.float32)
    scores = jnp.where(valid[None, :, None, None], scores, -jnp.inf)
    sink = sinks.astype(jnp.float32).reshape(A_KV_HEADS, A_GROUP)[:, :, None, None]
    m = jnp.maximum(scores.max(axis=-1, keepdims=True), sink)
    e = jnp.exp(scores - m)
    probs = e / (e.sum(axis=-1, keepdims=True) + jnp.exp(sink - m))
    out = jnp.einsum('bnkgij,bnjkd->bnikgd', probs.astype(v.dtype), vb)
    return out.reshape(b, s, A_WIDTH)


def diff_attn(q, k, v, lam, subln_g, lam_init):
    b, s = q.shape[0], q.shape[1]
    nb = s // BLOCK
    qb = jnp.moveaxis(q.reshape(b, nb, BLOCK, B_HEADS, 2, HEAD_DIM), 1, 0)
    slopes = alibi_slopes(B_HEADS)[:, None, None]
    kpos = jnp.arange(s)
    scale = HEAD_DIM ** -0.5

    def one_block(args):
        qblk, n = args
        qpos = n * BLOCK + jnp.arange(BLOCK)
        dist = (qpos[:, None] - kpos[None, :]).astype(jnp.float32)
        scores = jnp.einsum('bihcd,bjhcd->bchij', qblk, k).astype(jnp.float32) * scale
        scores = jnp.where(dist >= 0, scores - slopes * dist, -jnp.inf)
        pr = jax.nn.softmax(scores, axis=-1)
        w = pr[:, 0] - lam * pr[:, 1]
        return jnp.einsum('bhij,bjhe->bihe', w.astype(v.dtype), v)

    out = lax.map(one_block, (qb, jnp.arange(nb)))
    out = jnp.moveaxis(out, 0, 1).reshape(b, s, B_HEADS, B_VDIM)
    out = rmsnorm(out, subln_g) * (1.0 - lam_init)
    return out.reshape(b, s, B_WIDTH)


def forgetting_attn(q, k, v, f_logit):
    b, s = q.shape[0], q.shape[1]
    nb = s // BLOCK
    c = jnp.cumsum(jax.nn.log_sigmoid(f_logit.astype(jnp.float32)), axis=1)
    cT = jnp.moveaxis(c, 1, 2)
    qb = jnp.moveaxis(q.reshape(b, nb, BLOCK, C_HEADS, HEAD_DIM), 1, 0)
    cb = jnp.moveaxis(cT.reshape(b, C_HEADS, nb, BLOCK), 2, 0)
    kpos = jnp.arange(s)
    scale = HEAD_DIM ** -0.5

    def one_block(args):
        qblk, cq, n = args
        qpos = n * BLOCK + jnp.arange(BLOCK)
        causal = qpos[:, None] >= kpos[None, :]
        scores = (jnp.einsum('bihd,bjhd->bhij', qblk, k).astype(jnp.float32) * scale
                  + cq[..., None] - cT[:, :, None, :])
        scores = jnp.where(causal, scores, -jnp.inf)
        pr = jax.nn.softmax(scores, axis=-1)
        return jnp.einsum('bhij,bjhd->bihd', pr.astype(v.dtype), v)

    out = lax.map(one_block, (qb, cb, jnp.arange(nb)))
    return jnp.moveaxis(out, 0, 1).reshape(b, s, C_WIDTH)


def mixer_ab(h, w_in, w_out, sinks, lam_params, subln_g, lam_init):
    b, s, _ = h.shape
    z = h @ w_in
    qa, ka, va, ga, qd, kd, vd, gd = split_cols(z, AB_SPLITS)
    ya = sliding_window_sink_attn(qa.reshape(b, s, A_HEADS, HEAD_DIM),
                                  ka.reshape(b, s, A_KV_HEADS, HEAD_DIM),
                                  va.reshape(b, s, A_KV_HEADS, HEAD_DIM), sinks)
    ya = ya * jax.nn.silu(ga)
    lp = lam_params.astype(jnp.float32)
    lam = jnp.exp(jnp.sum(lp[0] * lp[1])) - jnp.exp(jnp.sum(lp[2] * lp[3])) + lam_init
    yb = diff_attn(qd.reshape(b, s, B_HEADS, 2, HEAD_DIM),
                   kd.reshape(b, s, B_HEADS, 2, HEAD_DIM),
                   vd.reshape(b, s, B_HEADS, B_VDIM), lam, subln_g, lam_init)
    yb = yb * jax.nn.silu(gd)
    return jnp.concatenate([ya, yb], axis=-1) @ w_out


def mixer_c(h, w_in, w_out, f_bias):
    b, s, _ = h.shape
    z = h @ w_in
    q, k, v, fz, g = split_cols(z, C_SPLITS)
    y = forgetting_attn(q.reshape(b, s, C_HEADS, HEAD_DIM),
                        k.reshape(b, s, C_HEADS, HEAD_DIM),
                        v.reshape(b, s, C_HEADS, HEAD_DIM), fz + f_bias)
    return (y * jax.nn.silu(g)) @ w_out


def setup_inputs(seed: int = 0) -> dict:
    key = jax.random.key(seed)
    ks = jax.random.split(key, 16)
    f32 = jnp.float32
    nrm = lambda k, shape, scale: jax.random.normal(k, shape, f32) * scale
    return {
        "x": nrm(ks[0], (BATCH, SEQ, D_MODEL), 1.0),
        "p": nrm(ks[1], (DEPTH, BATCH, SEQ, D_PLE), 1.0),
        "norm_g": 1.0 + nrm(ks[2], (DEPTH, D_MODEL), 0.02),
        "w_in_ab": nrm(ks[3], (N_EVEN, D_MODEL, AB_IN), D_MODEL ** -0.5),
        "w_out_ab": nrm(ks[4], (N_EVEN, AB_OUT, D_MODEL), AB_OUT ** -0.5),
        "attn_sinks": nrm(ks[5], (N_EVEN, A_HEADS), 0.5),
        "diff_lambda": nrm(ks[6], (N_EVEN, 4, HEAD_DIM), 0.1),
        "diff_subln_g": 1.0 + nrm(ks[7], (N_EVEN, B_VDIM), 0.02),
        "w_in_c": nrm(ks[8], (N_ODD, D_MODEL, C_IN), D_MODEL ** -0.5),
        "w_out_c": nrm(ks[9], (N_ODD, C_WIDTH, D_MODEL), C_WIDTH ** -0.5),
        "forget_bias": 3.0 + nrm(ks[10], (N_ODD, C_HEADS), 1.0),
        "ple_proj": nrm(ks[11], (DEPTH, D_PLE, D_MODEL), D_PLE ** -0.5),
        "ple_gate": nrm(ks[12], (DEPTH, D_MODEL, D_MODEL), D_MODEL ** -0.5),
        "ple_norm_g": 1.0 + nrm(ks[13], (DEPTH, D_MODEL), 0.02),
        "final_norm_g": 1.0 + nrm(ks[14], (D_MODEL,), 0.02),
    }


def reference(x, p, norm_g, w_in_ab, w_out_ab, attn_sinks, diff_lambda, diff_subln_g,
              w_in_c, w_out_c, forget_bias, ple_proj, ple_gate, ple_norm_g, final_norm_g):
    for i in range(DEPTH):
        h = rmsnorm(x, norm_g[i])
        j = i // 2
        if i % 2 == 0:
            lam_init = 0.8 - 0.6 * math.exp(-0.3 * i)
            y = mixer_ab(h, w_in_ab[j], w_out_ab[j], attn_sinks[j], diff_lambda[j],
                         diff_subln_g[j], lam_init)
        else:
            y = mixer_c(h, w_in_c[j], w_out_c[j], forget_bias[j])
        x = x + y
        gate = jax.nn.sigmoid(rmsnorm(x, ple_norm_g[i]) @ ple_gate[i])
        x = x + gate * (p[i] @ ple_proj[i])
    return rmsnorm(x, final_norm_g)
```

```python
import math
from contextlib import ExitStack
import numpy as np
import ml_dtypes
import concourse.bass as bass
import concourse.mybir as mybir
from concourse.bass_utils import run_bass_kernel_spmd

F32 = mybir.dt.float32
BF16 = mybir.dt.bfloat16
AF = mybir.ActivationFunctionType
ALU = mybir.AluOpType
ENGS = ["sync", "tensor", "vector", "scalar", "gpsimd"]
NCORES = 8
NT = 1024
D = 2048
KC = 16
EPS = 1e-6
BIG = 3.0e38


class T:
    __slots__ = ("name", "h", "ws", "rs")

    def __init__(self, name, h):
        self.name = name
        self.h = h
        self.ws = {}
        self.rs = {}

    def __getitem__(self, idx):
        return self.h[idx]


def I(name, *a, **k):
    return lambda e: getattr(e, name)(*a, **k)


class Prog:
    N_DMA_SEMS = 48

    def __init__(self, nc):
        self.nc = nc
        self.ops = {e: [] for e in ENGS}
        self.cnt = {e: 0 for e in ENGS}
        self.dma_rr = 0
        self.dma_tot = [0] * self.N_DMA_SEMS
        self.seen = {e: {} for e in ENGS}

    def _need(self, eng, key, val, waits):
        if self.seen[eng].get(key, 0) >= val:
            return
        if waits.get(key, 0) < val:
            waits[key] = val

    def _deps(self, eng, reads, writes, pe_accum):
        waits = {}
        for t in reads:
            for k, v in t.ws.items():
                self._need(eng, k, v, waits)
        for t in writes:
            for k, v in t.ws.items():
                if pe_accum and k == ("e", "tensor"):
                    continue
                self._need(eng, k, v, waits)
            for k, v in t.rs.items():
                if k == ("e", eng):
                    continue
                self._need(eng, k, v, waits)
        for k, v in waits.items():
            self.seen[eng][k] = v
        return waits

    def _mark(self, key, val, reads, writes):
        for t in reads:
            if t.rs.get(key, 0) < val:
                t.rs[key] = val
        for t in writes:
            t.ws = {key: val}
            t.rs = {}

    def op(self, eng, fn, reads=(), writes=(), pe_accum=False):
        waits = self._deps(eng, reads, writes, pe_accum)
        self.cnt[eng] += 1
        self.ops[eng].append((fn, waits, "c", None))
        self._mark(("e", eng), self.cnt[eng], reads, writes)
        return (("e", eng), self.cnt[eng])

    def dma(self, eng, fn, reads=(), writes=()):
        waits = self._deps(eng, reads, writes, False)
        k = self.dma_rr
        self.dma_rr = (self.dma_rr + 1) % self.N_DMA_SEMS
        key = ("d", k)
        if self.dma_tot[k] > 0 and self.seen[eng].get(key, 0) < self.dma_tot[k]:
            waits[key] = self.dma_tot[k]
            self.seen[eng][key] = self.dma_tot[k]
        self.dma_tot[k] += 16
        self.ops[eng].append((fn, waits, "d", k))
        self._mark(key, self.dma_tot[k], reads, writes)
        return (key, self.dma_tot[k])

    def wait_all(self, eng, toks):
        waits = {}
        for key, val in toks:
            self._need(eng, key, val, waits)
        for k, v in waits.items():
            self.seen[eng][k] = v
        self.ops[eng].append((None, waits, "w", None))

    def emit(self):
        nc = self.nc
        with ExitStack() as st:
            esem = {e: st.enter_context(nc.semaphore("es_" + e)) for e in ENGS}
            dsem = [st.enter_context(nc.semaphore("ds_%d" % i)) for i in range(self.N_DMA_SEMS)]
            block = st.enter_context(nc.Block())

            def mk(ename):
                def body(eng):
                    for fn, waits, kind, k in self.ops[ename]:
                        for key, val in waits.items():
                            sem = esem[key[1]] if key[0] == "e" else dsem[key[1]]
                            eng.wait_ge(sem, val)
                        if kind == "c":
                            fn(eng).then_inc(esem[ename], 1)
                        elif kind == "d":
                            fn(eng).then_inc(dsem[k], 16)
                return body

            block.sync(mk("sync"))
            block.tensor(mk("tensor"))
            block.vector(mk("vector"))
            block.scalar(mk("scalar"))
            block.gpsimd(mk("gpsimd"))


def build(stage=0):
    nc = bass.Bass("TRN2", target_bir_lowering=False)
    st = ExitStack()
    P = Prog(nc)

    in_names = []

    def ein(name, shape, dt=F32):
        in_names.append(name)
        return T(name, nc.dram_tensor(name, list(shape), dt, kind="ExternalInput").ap())

    ext_out = []

    def dint(name, shape, dt=BF16, prod=0, cons=0):
        if stage == 0 or prod == cons:
            h = nc.dram_tensor(name, list(shape), dt)
            return T(name, h[tuple(slice(None) for _ in shape)])
        if stage == prod:
            t = T(name, nc.dram_tensor(name, list(shape), dt, kind="ExternalOutput").ap())
            ext_out.append(t)
            return t
        if stage == cons:
            in_names.append(name)
            return T(name, nc.dram_tensor(name, list(shape), dt, kind="ExternalInput").ap())
        h = nc.dram_tensor(name, list(shape), dt)
        return T(name, h[tuple(slice(None) for _ in shape)])

    def sb(name, shape, dt):
        return T(name, st.enter_context(nc.sbuf_tensor("sb_" + name, list(shape), dt)))

    def ps(name):
        return T(name, st.enter_context(nc.psum_tensor(name, [128, 1024], F32)))

    x_in = ein("x", [NT, D])
    p_in = ein("p", [2, NT, 256])
    w_in_ab = ein("w_in_ab", [D, 6400])
    w_out_ab = ein("w_out_ab", [D, D])
    w_in_c = ein("w_in_c", [D, 8224])
    w_out_c = ein("w_out_c", [D, D])
    ple_proj = ein("ple_proj", [2, 256, D])
    ple_gate = ein("ple_gate", [2, D, D])
    gvec_in = ein("gvec", [5, 128, KC])
    sinks_in = ein("sinks", [128, 16])
    lam_in = ein("lam", [128, 256])
    subg_in = ein("subg", [128, 1])
    fb_in = ein("fbias", [32, 1])
    ident_in = ein("ident", [128, 128])
    sel_in = ein("sel", [128, 8])
    maskB_in = ein("maskB", [128, 8, 128], BF16)
    maskA_in = ein("maskA", [128, 9, 128], BF16)
    qaugA_in = ein("qaugA", [16, 9, NT], BF16)
    kaugA_in = ein("kaugA", [9, 8, NT], BF16)
    qaugB_in = ein("qaugB", [8, 4, NT], BF16)
    kaugB_in = ein("kaugB", [8, 4, 8, NT], BF16)
    ones3_in = ein("ones3", [3, 8, NT], BF16)
    rst_in = ein("rstm", [32, NT])
    out_t = T("out", nc.dram_tensor("out", [NT, D], F32, kind="ExternalOutput").ap()) if stage in (0, 3) else None

    qT0 = dint("qT0", [2048, NT], BF16, 1, 2); sg0 = dint("sg0", [2048, NT], BF16, 1, 2)
    kt0_loc = dint("kt0_loc", [1152, NT], BF16, 1, 9); kt0_all = dint("kt0_all", [8 * 1152, NT], BF16, 9, 2)
    v0b_loc = dint("v0b_loc", [1024, 1024], BF16, 1, 9); v0b_all = dint("v0b_all", [8 * 1024, 1024], BF16, 9, 2)
    v0a_loc = dint("v0a_loc", [256, 512], BF16, 1, 9); v0a_all = dint("v0a_all", [8 * 256, 512], BF16, 9, 2)
    qT1 = dint("qT1", [2048, NT], BF16, 2, 3); sg1 = dint("sg1", [2048, NT], BF16, 2, 3)
    kt1_loc = dint("kt1_loc", [2048, NT], BF16, 2, 9); kt1_all = dint("kt1_all", [8 * 2048, NT], BF16, 9, 3)
    v1_loc = dint("v1_loc", [4096, 512], BF16, 2, 9); v1_all = dint("v1_all", [8 * 4096, 512], BF16, 9, 3)
    w1_loc = dint("w1_loc", [32, NT], F32, 2, 3); w1_all = dint("w1_all", [8 * 32, NT], F32, 9, 3)
    xsav1 = dint("xsav1", [128, KC * NT], F32, 1, 2); xsav2 = dint("xsav2", [128, KC * NT], F32, 2, 3)
    qaug1 = dint("qaug1", [3, 32, NT]); kaug1 = dint("kaug1", [3, 32, 8, NT])

    xT = sb("xT", [128, KC, NT], F32)
    hT = sb("hT", [128, KC, NT], BF16)
    wts = [sb("wt%d" % i, [128, KC, 128], BF16) for i in range(3)]
    zts = [sb("zt%d" % i, [128, NT], BF16) for i in range(2)]
    vts = [sb("vt%d" % i, [128, 8, 128], BF16) for i in range(1)]
    ktr = [sb("kt%d" % i, [128, 2048], BF16) for i in range(3)]
    vtr = [sb("vr%d" % i, [128, 16, 128], BF16) for i in range(3)]
    qts = [sb("qt%d" % i, [128, 2, NT], BF16) for i in range(2)]
    gts = [sb("gt%d" % i, [128, NT], BF16) for i in range(2)]
    pts = [sb("pt%d" % i, [128, NT], BF16) for i in range(2)]
    fs = [sb("fs%d" % i, [128, NT], F32) for i in range(4)]
    bs_ = [sb("bs%d" % i, [128, NT], BF16) for i in range(2)]
    pT = qts[0]
    ppw = [sb("ppw%d" % i, [128, 2, 128], BF16) for i in range(2)]
    gv = sb("gv", [128, 5, KC], F32)
    ident = sb("ident", [128, 128], F32)
    ones_bf = sb("ones_bf", [128, 128], BF16)
    maskB = sb("maskB", [128, 8, 128], BF16)
    maskA = sb("maskA", [128, 9, 128], BF16)
    sm = sb("sm", [128, 768], F32)
    PS = [ps("ps%d" % i) for i in range(4)]

    C_ES = 0
    C_LAM = 16
    C_NLAM = 280
    C_SUBG = 281
    C_NFB = 282
    C_SEL = 284
    C_T = 296

    sq = "sync"

    P.dma(sq, I("dma_start", out=gv[:, :, :], in_=gvec_in[:, :, :].rearrange("g p k -> p g k")), writes=[gv])
    P.dma(sq, I("dma_start", out=ident[:, :], in_=ident_in[:, :]), writes=[ident])
    P.dma(sq, I("dma_start", out=maskB[:, :, :], in_=maskB_in[:, :, :]), writes=[maskB])
    P.dma(sq, I("dma_start", out=maskA[:, :, :], in_=maskA_in[:, :, :]), writes=[maskA])
    P.op("vector", I("memset", ones_bf[:, :], 1.0), writes=[ones_bf])
    P.dma(sq, I("dma_start", out=sm[:, C_ES:C_ES + 16], in_=sinks_in[:, :]), writes=[sm])
    P.dma(sq, I("dma_start", out=sm[:, C_LAM:C_LAM + 256], in_=lam_in[:, :]), writes=[sm])
    P.dma(sq, I("dma_start", out=sm[:, C_SUBG:C_SUBG + 1], in_=subg_in[:, :]), writes=[sm])
    P.dma(sq, I("dma_start", out=sm[0:32, C_NFB:C_NFB + 1], in_=fb_in[:, :]), writes=[sm])
    P.dma(sq, I("dma_start", out=sm[:, C_SEL:C_SEL + 8], in_=sel_in[:, :]), writes=[sm])
    P.op("scalar", I("activation", out=sm[:, C_ES:C_ES + 16], in_=sm[:, C_ES:C_ES + 16], func=AF.Exp), reads=[sm], writes=[sm])
    lam_init = 0.8 - 0.6 * math.exp(-0.3 * 0)
    P.op("vector", I("tensor_tensor", out=sm[:, C_T:C_T + 64], in0=sm[:, C_LAM:C_LAM + 64], in1=sm[:, C_LAM + 64:C_LAM + 128], op=ALU.mult), reads=[sm], writes=[sm])
    P.op("vector", I("tensor_reduce", out=sm[:, C_T + 128:C_T + 129], in_=sm[:, C_T:C_T + 64], axis=mybir.AxisListType.X, op=ALU.add), reads=[sm], writes=[sm])
    P.op("vector", I("tensor_tensor", out=sm[:, C_T:C_T + 64], in0=sm[:, C_LAM + 128:C_LAM + 192], in1=sm[:, C_LAM + 192:C_LAM + 256], op=ALU.mult), reads=[sm], writes=[sm])
    P.op("vector", I("tensor_reduce", out=sm[:, C_T + 129:C_T + 130], in_=sm[:, C_T:C_T + 64], axis=mybir.AxisListType.X, op=ALU.add), reads=[sm], writes=[sm])
    P.op("scalar", I("activation", out=sm[:, C_T + 128:C_T + 130], in_=sm[:, C_T + 128:C_T + 130], func=AF.Exp), reads=[sm], writes=[sm])
    P.op("vector", I("scalar_tensor_tensor", out=sm[:, C_NLAM:C_NLAM + 1], in0=sm[:, C_T + 129:C_T + 130], scalar=-lam_init,
                     in1=sm[:, C_T + 128:C_T + 129], op0=ALU.add, op1=ALU.subtract), reads=[sm], writes=[sm])
    P.op("vector", I("tensor_scalar", out=sm[:, C_SUBG:C_SUBG + 1], in0=sm[:, C_SUBG:C_SUBG + 1], scalar1=1.0 - lam_init, scalar2=None, op0=ALU.mult), reads=[sm], writes=[sm])
    P.op("vector", I("tensor_scalar", out=sm[0:32, C_NFB:C_NFB + 1], in0=sm[0:32, C_NFB:C_NFB + 1], scalar1=-1.0, scalar2=None, op0=ALU.mult), reads=[sm], writes=[sm])

    def save_x(dst):
        P.dma(sq, I("dma_start", out=dst[:, :], in_=xT[:, :, :].rearrange("p k t -> p (k t)")), reads=[xT], writes=[dst])

    if stage == 2:
        P.dma(sq, I("dma_start", out=xT[:, :, :].rearrange("p k t -> p (k t)"), in_=xsav1[:, :]), reads=[xsav1], writes=[xT])
    if stage == 3:
        P.dma(sq, I("dma_start", out=xT[:, :, :].rearrange("p k t -> p (k t)"), in_=xsav2[:, :]), reads=[xsav2], writes=[xT])

    if stage in (0, 1):
        for s in range(8):
            for hf in range(2):
                f = fs[hf]
                P.dma(sq, I("dma_start", out=f[:, :], in_=x_in[s * 128:(s + 1) * 128, hf * 1024:(hf + 1) * 1024]), writes=[f])
                pst = PS[(s * 2 + hf) % 4]
                for j in range(8):
                    P.op("tensor", I("transpose", out=pst[:, j * 128:(j + 1) * 128], in_=f[:, j * 128:(j + 1) * 128], identity=ident[:, :]),
                         reads=[f, ident], writes=[pst], pe_accum=True)
                P.op("vector", I("tensor_copy", out=xT[:, hf * 8:(hf + 1) * 8, s * 128:(s + 1) * 128],
                                 in_=pst[:, :].rearrange("p (j t) -> p j t", t=128)), reads=[pst], writes=[xT])

    def rstd_bc(dst):
        pst = PS[0]
        for kc in range(KC):
            b = bs_[kc % 2]
            P.op("scalar", I("activation", out=b[:, :], in_=xT[:, kc, :], func=AF.Square), reads=[xT], writes=[b])
            for hf in range(2):
                P.op("tensor", I("matmul", pst[:, hf * 512:(hf + 1) * 512], lhsT=ones_bf[:, :], rhs=b[:, hf * 512:(hf + 1) * 512],
                                 start=(kc == 0), stop=(kc == KC - 1)), reads=[ones_bf, b], writes=[pst], pe_accum=True)
        P.op("scalar", I("activation", out=dst[:, :], in_=pst[:, :], func=AF.Sqrt, bias=EPS_AP(), scale=1.0 / D), reads=[pst, sm], writes=[dst])
        P.op("vector", I("reciprocal", out=dst[:, :], in_=dst[:, :]), reads=[dst], writes=[dst])

    def EPS_AP():
        return sm[:, C_T + 140:C_T + 141]

    P.op("vector", I("memset", sm[:, C_T + 140:C_T + 141], EPS), writes=[sm])
    P.op("vector", I("memset", sm[:, C_T + 141:C_T + 142], 1.0), writes=[sm])

    def norm_to_hT(gidx):
        r = fs[3]
        rstd_bc(r)
        for kc in range(KC):
            P.op("vector", I("scalar_tensor_tensor", out=hT[:, kc, :], in0=xT[:, kc, :], scalar=gv[:, gidx, kc:kc + 1], in1=r[:, :],
                             op0=ALU.mult, op1=ALU.mult), reads=[xT, gv, r], writes=[hT])

    wq = "gpsimd"
    wstate = {"i": 0}

    def load_w(w_t, col0, n):
        wt = wts[wstate["i"] % 3]
        wstate["i"] += 1
        P.dma(wq, I("dma_start", out=wt[:, :, 0:n], in_=w_t.h.rearrange("(kc p) c -> p kc c", p=128)[:, :, col0:col0 + n]), writes=[wt])
        return wt

    def proj_groups(groups, w_t, rhsT, handler):
        pend = []
        for gi in range(min(2, len(groups))):
            pend.append(load_w(w_t, groups[gi][0], groups[gi][1]))
        for gi, (col0, n, info) in enumerate(groups):
            wt = pend.pop(0)
            if gi + 2 < len(groups):
                pend.append(load_w(w_t, groups[gi + 2][0], groups[gi + 2][1]))
            pst = PS[gi % 2]
            for hf in range(2):
                for kc in range(KC):
                    P.op("tensor", I("matmul", pst[0:n, hf * 512:(hf + 1) * 512], lhsT=wt[:, kc, 0:n], rhs=rhsT[:, kc, hf * 512:(hf + 1) * 512],
                                     start=(kc == 0), stop=(kc == KC - 1)), reads=[wt, rhsT], writes=[pst], pe_accum=True)
            handler(gi, pst, n, info)

    zstate = {"i": 0}

    def evac_store(pst, n, func, scale, dst_t, row0):
        zt = zts[zstate["i"] % 2]
        zstate["i"] += 1
        P.op("scalar", I("activation", out=zt[0:n, :], in_=pst[0:n, :], func=func, scale=scale), reads=[pst], writes=[zt])
        P.dma(sq, I("dma_start", out=dst_t[row0:row0 + n, :], in_=zt[0:n, :]), reads=[zt], writes=[dst_t])

    def evac_v(pst, dst_ap_fn, dst_t):
        f = fs[0]
        P.op("scalar", I("activation", out=f[:, :], in_=pst[:, :], func=AF.Copy), reads=[pst], writes=[f])
        p2 = PS[2]
        for s in range(8):
            P.op("tensor", I("transpose", out=p2[:, s * 128:(s + 1) * 128], in_=f[:, s * 128:(s + 1) * 128], identity=ident[:, :]),
                 reads=[f, ident], writes=[p2], pe_accum=True)
        vt = vts[0]
        P.op("vector", I("tensor_copy", out=vt[:, :, :], in_=p2[:, :].rearrange("p (s f) -> p s f", f=128)), reads=[p2], writes=[vt])
        for (dst_ap, src_ap) in dst_ap_fn(vt):
            P.dma(sq, I("dma_start", out=dst_ap, in_=src_ap), reads=[vt], writes=[dst_t])

    def allgather(loc, allt):
        P.op("gpsimd", I("collective_compute", "AllGather", ALU.bypass, replica_groups=[list(range(NCORES))],
                         ins=[loc.h], outs=[allt.h]), reads=[loc], writes=[allt])

    ring = {"k": 0, "p": 0, "s": 0}

    def next_kv():
        i = ring["k"] % 3
        ring["k"] += 1
        return ktr[i], vtr[i]

    def s_exp(kt, kslot, K, q_ap, q_t, n):
        S = PS[ring["s"] % 2]
        ring["s"] += 1
        return S

    def attn_full(K, qt, qrows_loader, kv_loader, vmode, O, Sm):
        for qc in range(4):
            kt, vt = next_kv()
            kv_loader(qc, kt, vt)
            for J in range(16 * qc, 16 * qc + 16):
                slot = (J % 8) * 2 + (J // 8 - 2 * qc)
                lo = (J // 8) * 128
                pieces = []
                if lo < 512:
                    pieces.append((lo, 512))
                pieces.append((max(lo, 512), 1024))
                S = PS[ring["s"] % 2]
                ring["s"] += 1
                for (a, b) in pieces:
                    P.op("tensor", I("matmul", S[:, a:b], lhsT=kt[0:K, slot * 128:(slot + 1) * 128], rhs=qt[0:K, 0, a:b], start=True, stop=True),
                         reads=[kt, qt], writes=[S], pe_accum=True)
                pt = pts[ring["p"] % 2]
                ring["p"] += 1
                P.op("scalar", I("activation", out=pt[:, lo:1024], in_=S[:, lo:1024], func=AF.Exp), reads=[S], writes=[pt])
                P.op("vector", I("tensor_tensor", out=pt[:, lo:lo + 128], in0=pt[:, lo:lo + 128], in1=maskB[:, J % 8, :], op=ALU.min),
                     reads=[pt, maskB], writes=[pt])
                for (a, b) in pieces:
                    P.op("tensor", I("matmul", O[:, a:b], lhsT=vt[:, slot, :], rhs=pt[:, a:b], start=(J == 0), stop=(J == 63)),
                         reads=[vt, pt], writes=[O], pe_accum=True)
                    if vmode == "b":
                        P.op("tensor", I("matmul", Sm[:, a:b], lhsT=ones_bf[:, :], rhs=pt[:, a:b], start=(J == 0), stop=(J == 63)),
                             reads=[ones_bf, pt], writes=[Sm], pe_accum=True)

    def set_v_ones():
        for vt in vtr:
            P.op("vector", I("memset", vt[:, :, 64:128], 1.0), writes=[vt])

    if stage in (0, 1):
        norm_to_hT(0)
    g0 = []
    for j in range(8):
        g0.append((0 + j * 128, 128, ("q", qT0, j * 128)))
    g0.append((1024, 128, ("k", kt0_loc, 1024)))
    g0.append((1152, 128, ("va",)))
    for j in range(8):
        g0.append((1280 + j * 128, 128, ("g", sg0, j * 128)))
    for j in range(8):
        g0.append((2304 + j * 128, 128, ("q", qT0, 1024 + j * 128)))
    for j in range(8):
        g0.append((3328 + j * 128, 128, ("k", kt0_loc, j * 128)))
    for j in range(8):
        g0.append((4352 + j * 128, 128, ("vb", j)))
    for j in range(8):
        g0.append((5376 + j * 128, 128, ("g", sg0, 1024 + j * 128)))

    def h0(gi, pst, n, info):
        k = info[0]
        if k == "q":
            evac_store(pst, n, AF.Copy, 0.125, info[1], info[2])
        elif k == "k":
            evac_store(pst, n, AF.Copy, 1.0, info[1], info[2])
        elif k == "g":
            evac_store(pst, n, AF.Silu, 1.0, info[1], info[2])
        elif k == "vb":
            hd = info[1]
            evac_v(pst, lambda vt: [(v0b_loc[hd * 128:(hd + 1) * 128, :].rearrange("p (s d) -> p s d", d=128), vt[:, :, :])], v0b_loc)
        elif k == "va":
            evac_v(pst, lambda vt: [(v0a_loc[g * 128:(g + 1) * 128, :].rearrange("p (s d) -> p s d", d=64), vt[:, :, g * 64:(g + 1) * 64]) for g in range(2)], v0a_loc)

    if stage in (0, 1):
        proj_groups(g0, w_in_ab, hT, h0)
        if stage == 1:
            save_x(xsav1)
    if stage == 0:
        allgather(kt0_loc, kt0_all)
    if stage == 0:
        allgather(v0b_loc, v0b_all)
    if stage == 0:
        allgather(v0a_loc, v0a_all)

    kt0v = kt0_all.h.rearrange("(r m) t -> m r t", r=8)
    v0bv = v0b_all.h.rearrange("(r h p) (s d) -> h p r s d", r=8, h=8, d=128)
    v0av = v0a_all.h.rearrange("(r g p) (s d) -> g p r s d", r=8, g=2, d=64)
    yT = hT

    if stage in (0, 2):
        set_v_ones()
        KA = 73
        ai = 0
        for g in range(2):
            for dd in range(4):
                h0_ = g * 8 + dd * 2
                qt = qts[(g * 4 + dd) % 2]
                P.dma(sq, I("dma_start", out=qt[0:64, :, :], in_=qT0[h0_ * 64:(h0_ + 2) * 64, :].rearrange("(i d) t -> d i t", d=64)), reads=[qT0], writes=[qt])
                P.dma(sq, I("dma_start", out=qt[64:73, :, :], in_=qaugA_in[h0_:h0_ + 2, :, :].rearrange("i r t -> r i t")), writes=[qt])
                gt = gts[(g * 4 + dd) % 2]
                P.dma(sq, I("dma_start", out=gt[:, :], in_=sg0[h0_ * 64:(h0_ + 2) * 64, :]), reads=[sg0], writes=[gt])
                for s in range(8):
                    kt, vt = next_kv()
                    rows = slice(1024 + g * 64, 1024 + (g + 1) * 64)
                    if s > 0:
                        P.dma(sq, I("dma_start", out=kt[0:64, 0:128], in_=kt0v[rows, 7, (s - 1) * 128:s * 128]), reads=[kt0_all], writes=[kt])
                        P.dma(sq, I("dma_start", out=kt[64:73, 0:128], in_=kaugA_in[:, 7, (s - 1) * 128:s * 128]), writes=[kt])
                        P.dma(sq, I("dma_start", out=vt[:, 0, 0:64], in_=v0av[g, :, 7, s - 1, :]), reads=[v0a_all], writes=[vt])
                    P.dma(sq, I("dma_start", out=kt[0:64, 128:1152].rearrange("m (r t) -> m r t", t=128), in_=kt0v[rows, :, s * 128:(s + 1) * 128]), reads=[kt0_all], writes=[kt])
                    P.dma(sq, I("dma_start", out=kt[64:73, 128:1152].rearrange("m (r t) -> m r t", t=128), in_=kaugA_in[:, :, s * 128:(s + 1) * 128]), writes=[kt])
                    P.dma(sq, I("dma_start", out=vt[:, 1:9, 0:64], in_=v0av[g, :, :, s, :]), reads=[v0a_all], writes=[vt])
                    O = PS[2 + (ai % 2)]
                    first = 1 if s == 0 else 0
                    for idx in range(first, 9):
                        S = PS[ring["s"] % 2]
                        ring["s"] += 1
                        P.op("tensor", I("matmul", S[:, 0:256], lhsT=kt[0:KA, idx * 128:(idx + 1) * 128], rhs=qt[0:KA, :, s * 128:(s + 1) * 128], start=True, stop=True),
                             reads=[kt, qt], writes=[S], pe_accum=True)
                        pt = pts[ring["p"] % 2]
                        ring["p"] += 1
                        P.op("scalar", I("activation", out=pt[:, 0:256], in_=S[:, 0:256], func=AF.Exp), reads=[S], writes=[pt])
                        for i in range(2):
                            P.op("vector", I("tensor_tensor", out=pt[:, i * 128:(i + 1) * 128], in0=pt[:, i * 128:(i + 1) * 128], in1=maskA[:, idx, :], op=ALU.min),
                                 reads=[pt, maskA], writes=[pt])
                        P.op("tensor", I("matmul", O[:, 0:256], lhsT=vt[:, idx, :], rhs=pt[:, 0:256], start=(idx == first), stop=(idx == 8)),
                             reads=[vt, pt], writes=[O], pe_accum=True)
                    rc = fs[0]
                    for i in range(2):
                        h = h0_ + i
                        pb = i * 64
                        P.op("vector", I("tensor_scalar", out=rc[pb:pb + 64, 0:128], in0=O[64:128, i * 128:(i + 1) * 128],
                                         scalar1=sm[64:128, C_ES + h:C_ES + h + 1], scalar2=None, op0=ALU.add), reads=[O, sm], writes=[rc])
                        P.op("vector", I("reciprocal", out=rc[pb:pb + 64, 0:128], in_=rc[pb:pb + 64, 0:128]), reads=[rc], writes=[rc])
                        P.op("vector", I("tensor_tensor", out=rc[pb:pb + 64, 0:128], in0=rc[pb:pb + 64, 0:128],
                                         in1=gt[pb:pb + 64, s * 128:(s + 1) * 128], op=ALU.mult), reads=[rc, gt], writes=[rc])
                        hb = (h % 2) * 64
                        P.op("vector", I("tensor_tensor", out=yT[hb:hb + 64, h // 2, s * 128:(s + 1) * 128], in0=O[0:64, i * 128:(i + 1) * 128],
                                         in1=rc[pb:pb + 64, 0:128], op=ALU.mult), reads=[O, rc], writes=[yT])
                    ai += 1

        KB = 68
        for hd in range(8):
            gt = gts[hd % 2]
            P.dma(sq, I("dma_start", out=gt[:, :], in_=sg0[1024 + hd * 128:1024 + (hd + 1) * 128, :]), reads=[sg0], writes=[gt])
            for c in range(2):
                u = hd * 2 + c
                qt = qts[u % 2]
                P.dma(sq, I("dma_start", out=qt[0:64, 0, :], in_=qT0[1024 + u * 64:1024 + (u + 1) * 64, :]), reads=[qT0], writes=[qt])
                P.dma(sq, I("dma_start", out=qt[64:68, 0, :], in_=qaugB_in[hd, :, :]), writes=[qt])

                def kvl(qc, kt, vt, u=u, hd=hd):
                    P.dma(sq, I("dma_start", out=kt[0:64, :].rearrange("m (r t) -> m r t", t=256), in_=kt0v[u * 64:(u + 1) * 64, :, qc * 256:(qc + 1) * 256]), reads=[kt0_all], writes=[kt])
                    P.dma(sq, I("dma_start", out=kt[64:68, :].rearrange("m (r t) -> m r t", t=256), in_=kaugB_in[hd, :, :, qc * 256:(qc + 1) * 256]), writes=[kt])
                    for r in range(8):
                        P.dma(sq, I("dma_start", out=vt[:, 2 * r:2 * r + 2, :], in_=v0bv[hd, :, r, 2 * qc:2 * qc + 2, :]), reads=[v0b_all], writes=[vt])

                O, Sm = PS[2], PS[3]
                attn_full(KB, qt, None, kvl, "b", O, Sm)
                rc = fs[0]
                P.op("vector", I("reciprocal", out=rc[:, :], in_=Sm[:, :]), reads=[Sm], writes=[rc])
                un = fs[1 + c]
                P.op("vector", I("tensor_tensor", out=un[:, :], in0=O[:, :], in1=rc[:, :], op=ALU.mult), reads=[O, rc], writes=[un])
            U = fs[1]
            P.op("vector", I("scalar_tensor_tensor", out=U[:, :], in0=fs[2][:, :], scalar=sm[:, C_NLAM:C_NLAM + 1], in1=fs[1][:, :], op0=ALU.mult, op1=ALU.add),
                 reads=[fs[1], fs[2], sm], writes=[U])
            b = bs_[0]
            P.op("scalar", I("activation", out=b[:, :], in_=U[:, :], func=AF.Square), reads=[U], writes=[b])
            pst = PS[0]
            for hf in range(2):
                P.op("tensor", I("matmul", pst[:, hf * 512:(hf + 1) * 512], lhsT=ones_bf[:, :], rhs=b[:, hf * 512:(hf + 1) * 512], start=True, stop=True),
                     reads=[ones_bf, b], writes=[pst], pe_accum=True)
            r = fs[0]
            P.op("scalar", I("activation", out=r[:, :], in_=pst[:, :], func=AF.Sqrt, bias=EPS_AP(), scale=1.0 / 128), reads=[pst, sm], writes=[r])
            P.op("vector", I("reciprocal", out=r[:, :], in_=r[:, :]), reads=[r], writes=[r])
            P.op("vector", I("scalar_tensor_tensor", out=U[:, :], in0=U[:, :], scalar=sm[:, C_SUBG:C_SUBG + 1], in1=r[:, :], op0=ALU.mult, op1=ALU.mult),
                 reads=[U, sm, r], writes=[U])
            P.op("vector", I("tensor_tensor", out=yT[:, 8 + hd, :], in0=U[:, :], in1=gt[:, :], op=ALU.mult), reads=[U, gt], writes=[yT])

    def phase_R(li, w_out_t):
        groups = [(fc * 128, 128, fc) for fc in range(KC)]

        def hres(gi, pst, n, fc):
            P.op("vector", I("tensor_tensor", out=xT[:, fc, :], in0=pst[:, :], in1=xT[:, fc, :], op=ALU.add), reads=[pst, xT], writes=[xT])

        proj_groups(groups, w_out_t, yT, hres)
        norm_to_hT(2 + li)
        for s0 in range(0, 8, 4):
            f = fs[0]
            P.dma(sq, I("dma_start", out=f[:, :].rearrange("p (s c) -> p s c", c=256), in_=p_in[li, s0 * 128:(s0 + 4) * 128, :].rearrange("(s p) c -> p s c", p=128)), writes=[f])
            pst = PS[2]
            for k2 in range(2):
                for sl in range(4):
                    j = k2 * 4 + sl
                    P.op("tensor", I("transpose", out=pst[:, j * 128:(j + 1) * 128], in_=f[:, sl * 256 + k2 * 128:sl * 256 + (k2 + 1) * 128], identity=ident[:, :]),
                         reads=[f, ident], writes=[pst], pe_accum=True)
            P.op("vector", I("tensor_copy", out=pT[:, :, s0 * 128:(s0 + 4) * 128], in_=pst[:, :].rearrange("p (k t) -> p k t", k=2)), reads=[pst], writes=[pT])
        pwv = ple_proj.h[li].rearrange("(k p) c -> p k c", p=128)
        gw = T("pg%d" % li, ple_gate.h[li])

        def hgate(gi, pst, n, fc):
            pw = ppw[gi % 2]
            P.dma(wq, I("dma_start", out=pw[:, :, :], in_=pwv[:, :, fc * 128:(fc + 1) * 128]), writes=[pw])
            p2 = PS[2 + gi % 2]
            for hf in range(2):
                for k2 in range(2):
                    P.op("tensor", I("matmul", p2[:, hf * 512:(hf + 1) * 512], lhsT=pw[:, k2, :], rhs=pT[:, k2, hf * 512:(hf + 1) * 512], start=(k2 == 0), stop=(k2 == 1)),
                         reads=[pw, pT], writes=[p2], pe_accum=True)
            gs = fs[gi % 2]
            P.op("scalar", I("activation", out=gs[:, :], in_=pst[:, :], func=AF.Sigmoid), reads=[pst], writes=[gs])
            P.op("vector", I("tensor_tensor", out=gs[:, :], in0=p2[:, :], in1=gs[:, :], op=ALU.mult), reads=[p2, gs], writes=[gs])
            P.op("gpsimd", I("tensor_tensor", out=xT[:, fc, :], in0=xT[:, fc, :], in1=gs[:, :], op=ALU.add), reads=[xT, gs], writes=[xT])

        proj_groups(groups, gw, hT, hgate)

    if stage in (0, 2):
        phase_R(0, w_out_ab)

    if True:
        if stage in (0, 2):
            norm_to_hT(1)
        g1 = []
        for j in range(16):
            g1.append((j * 128, 128, ("q", qT1, j * 128)))
        for j in range(16):
            g1.append((2048 + j * 128, 128, ("k", kt1_loc, j * 128)))
        for j in range(16):
            g1.append((4096 + j * 128, 128, ("v", j)))
        g1.append((6144, 32, ("f",)))
        for j in range(16):
            g1.append((6176 + j * 128, 128, ("g", sg1, j * 128)))

        rstm = fs[2]
        P.dma(sq, I("dma_start", out=rstm[0:32, :], in_=rst_in[:, :]), writes=[rstm])

        def h1(gi, pst, n, info):
            k = info[0]
            if k == "q":
                evac_store(pst, n, AF.Copy, 0.125, info[1], info[2])
            elif k == "k":
                evac_store(pst, n, AF.Copy, 1.0, info[1], info[2])
            elif k == "g":
                evac_store(pst, n, AF.Silu, 1.0, info[1], info[2])
            elif k == "v":
                j = info[1]
                evac_v(pst, lambda vt: [(v1_loc[(2 * j + e) * 128:(2 * j + e + 1) * 128, :].rearrange("p (s d) -> p s d", d=64), vt[:, :, e * 64:(e + 1) * 64]) for e in range(2)], v1_loc)
            elif k == "f":
                f = fs[1]
                P.op("scalar", I("activation", out=f[0:32, :], in_=pst[0:32, :], func=AF.Exp, bias=sm[0:32, C_NFB:C_NFB + 1], scale=-1.0), reads=[pst, sm], writes=[f])
                P.op("scalar", I("activation", out=f[0:32, :], in_=f[0:32, :], func=AF.Ln, bias=sm[0:32, C_T + 141:C_T + 142], scale=1.0), reads=[f, sm], writes=[f])
                P.op("vector", I("tensor_scalar", out=f[0:32, :], in0=f[0:32, :], scalar1=-1.0, scalar2=None, op0=ALU.mult), reads=[f], writes=[f])
                P.op("vector", I("tensor_tensor_scan", out=f[0:32, :], data0=rstm[0:32, :], data1=f[0:32, :], initial=0.0, op0=ALU.mult, op1=ALU.add),
                     reads=[f, rstm], writes=[f])
                P.dma(sq, I("dma_start", out=w1_loc[:, :], in_=f[0:32, :]), reads=[f], writes=[w1_loc])

        if stage in (0, 2):
            proj_groups(g1, w_in_c, hT, h1)
            if stage == 2:
                save_x(xsav2)
        if stage == 0:
            allgather(kt1_loc, kt1_all)
        if stage == 0:
            allgather(v1_loc, v1_all)
        if stage == 0:
            allgather(w1_loc, w1_all)

        if stage in (0, 3):
            w1v = w1_all.h.rearrange("(r h) t -> h r t", r=8)
            B1, B2, B3, B4 = 512, 576, 640, 704
            for r in range(8):
                c_ = fs[r % 2]
                P.dma(sq, I("dma_start", out=c_[0:32, :], in_=w1v[:, r, :]), reads=[w1_all], writes=[c_])
                P.op("vector", I("tensor_copy", out=sm[0:32, B1 + r * 8:B1 + r * 8 + 8], in_=c_[0:32, :].rearrange("h (s t) -> h s t", t=128)[:, :, 127]), reads=[c_], writes=[sm])
            P.op("vector", I("tensor_copy", out=sm[0:32, B2:B2 + 64].rearrange("h (s r) -> h s r", r=8), in_=sm[0:32, B1:B1 + 64].rearrange("h (r s) -> h s r", s=8)), reads=[sm], writes=[sm])
            P.op("vector", I("tensor_tensor_scan", out=sm[0:32, B3:B3 + 64], data0=rstm[0:32, 1:65], data1=sm[0:32, B2:B2 + 64], initial=0.0, op0=ALU.mult, op1=ALU.add), reads=[sm, rstm], writes=[sm])
            P.op("vector", I("tensor_tensor", out=sm[0:32, B3:B3 + 64], in0=sm[0:32, B3:B3 + 64], in1=sm[0:32, B2:B2 + 64], op=ALU.subtract), reads=[sm], writes=[sm])
            offs = sm[0:32, B3:B3 + 64].rearrange("h (s r) -> h s r", r=8)

            def split3_store(c_, r1, dst_fn, dst_t):
                pz = [bs_[0], bs_[1], pts[0]]
                P.op("vector", I("tensor_copy", out=pz[0][0:32, :], in_=c_[0:32, :]), reads=[c_], writes=[pz[0]])
                P.op("vector", I("tensor_tensor", out=r1[0:32, :], in0=c_[0:32, :], in1=pz[0][0:32, :], op=ALU.subtract), reads=[c_, pz[0]], writes=[r1])
                P.op("vector", I("tensor_copy", out=pz[1][0:32, :], in_=r1[0:32, :]), reads=[r1], writes=[pz[1]])
                P.op("vector", I("tensor_tensor", out=r1[0:32, :], in0=r1[0:32, :], in1=pz[1][0:32, :], op=ALU.subtract), reads=[r1, pz[1]], writes=[r1])
                P.op("vector", I("tensor_copy", out=pz[2][0:32, :], in_=r1[0:32, :]), reads=[r1], writes=[pz[2]])
                for k in range(3):
                    P.dma(sq, I("dma_start", out=dst_fn(k), in_=pz[k][0:32, :]), reads=[pz[k]], writes=[dst_t])

            for r in range(8):
                c_ = fs[0]; r1 = fs[1]
                P.dma(sq, I("dma_start", out=c_[0:32, :], in_=w1v[:, r, :]), reads=[w1_all], writes=[c_])
                for s in range(8):
                    P.op("vector", I("tensor_scalar", out=c_[0:32, s * 128:(s + 1) * 128], in0=c_[0:32, s * 128:(s + 1) * 128], scalar1=offs[:, s, r:r + 1], scalar2=-1.0,
                                     op0=ALU.add, op1=ALU.mult), reads=[c_, sm], writes=[c_])
                split3_store(c_, r1, lambda k, r=r: kaug1[k, :, r, :], kaug1)
            P.op("vector", I("tensor_copy", out=sm[0:32, B1:B1 + 64], in_=sm[0:32, B3:B3 + 64]), reads=[sm], writes=[sm])
            for s in range(8):
                P.op("vector", I("tensor_tensor", out=sm[0:32, B1 + s * 8:B1 + s * 8 + 8], in0=sm[0:32, B1 + s * 8:B1 + s * 8 + 8], in1=sm[0:32, C_SEL:C_SEL + 8], op=ALU.mult), reads=[sm], writes=[sm])
                P.op("vector", I("tensor_reduce", out=sm[0:32, B4 + s:B4 + s + 1], in_=sm[0:32, B1 + s * 8:B1 + s * 8 + 8], axis=mybir.AxisListType.X, op=ALU.add), reads=[sm], writes=[sm])
            c_ = fs[0]; r1 = fs[1]
            P.dma(sq, I("dma_start", out=c_[0:32, :], in_=w1_loc[:, :]), reads=[w1_loc], writes=[c_])
            for s in range(8):
                P.op("vector", I("tensor_scalar", out=c_[0:32, s * 128:(s + 1) * 128], in0=c_[0:32, s * 128:(s + 1) * 128], scalar1=sm[0:32, B4 + s:B4 + s + 1], scalar2=None,
                                 op0=ALU.add), reads=[c_, sm], writes=[c_])
            split3_store(c_, r1, lambda k: qaug1[k, :, :], qaug1)

            kt1v = kt1_all.h.rearrange("(r m) t -> m r t", r=8)
            v1v = v1_all.h.rearrange("(r h p) (s d) -> h p r s d", r=8, h=32, d=64)
            set_v_ones()
            KF = 70
            for hd in range(32):
                qt = qts[hd % 2]
                P.dma(sq, I("dma_start", out=qt[0:64, 0, :], in_=qT1[hd * 64:(hd + 1) * 64, :]), reads=[qT1], writes=[qt])
                P.dma(sq, I("dma_start", out=qt[64:67, 0, :], in_=qaug1[:, hd, :]), reads=[qaug1], writes=[qt])
                P.dma(sq, I("dma_start", out=qt[67:70, 0, :], in_=ones3_in[:, 0, :]), writes=[qt])
                gt = gts[hd % 2]
                P.dma(sq, I("dma_start", out=gt[64:128, :], in_=sg1[hd * 64:(hd + 1) * 64, :]), reads=[sg1], writes=[gt])

                def kvl(qc, kt, vt, hd=hd):
                    P.dma(sq, I("dma_start", out=kt[0:64, :].rearrange("m (r t) -> m r t", t=256), in_=kt1v[hd * 64:(hd + 1) * 64, :, qc * 256:(qc + 1) * 256]), reads=[kt1_all], writes=[kt])
                    P.dma(sq, I("dma_start", out=kt[64:67, :].rearrange("m (r t) -> m r t", t=256), in_=ones3_in[:, :, qc * 256:(qc + 1) * 256]), writes=[kt])
                    P.dma(sq, I("dma_start", out=kt[67:70, :].rearrange("m (r t) -> m r t", t=256), in_=kaug1[:, hd, :, qc * 256:(qc + 1) * 256]), reads=[kaug1], writes=[kt])
                    for r in range(8):
                        P.dma(sq, I("dma_start", out=vt[:, 2 * r:2 * r + 2, 0:64], in_=v1v[hd, :, r, 2 * qc:2 * qc + 2, :]), reads=[v1_all], writes=[vt])

                O = PS[2 + (hd % 2)]
                attn_full(KF, qt, None, kvl, "f", O, None)
                rc = fs[hd % 2]
                P.op("vector", I("reciprocal", out=rc[64:128, :], in_=O[64:128, :]), reads=[O], writes=[rc])
                P.op("vector", I("tensor_tensor", out=rc[64:128, :], in0=rc[64:128, :], in1=gt[64:128, :], op=ALU.mult), reads=[rc, gt], writes=[rc])
                hb = (hd % 2) * 64
                P.op("vector", I("tensor_tensor", out=yT[hb:hb + 64, hd // 2, :], in0=O[0:64, :], in1=rc[64:128, :], op=ALU.mult), reads=[O, rc], writes=[yT])

            phase_R(1, w_out_c)

    if stage in (0, 3):
        r = fs[3]
        rstd_bc(r)
        outv = out_t.h.rearrange("(s p) f -> p s f", p=128)
        toks = []
        for kc in range(KC):
            o = fs[kc % 2]
            P.op("vector", I("scalar_tensor_tensor", out=o[:, :], in0=xT[:, kc, :], scalar=gv[:, 4, kc:kc + 1], in1=r[:, :], op0=ALU.mult, op1=ALU.mult),
                 reads=[xT, gv, r], writes=[o])
            pst = PS[kc % 4]
            for s in range(8):
                P.op("tensor", I("transpose", out=pst[:, s * 128:(s + 1) * 128], in_=o[:, s * 128:(s + 1) * 128], identity=ident[:, :]),
                     reads=[o, ident], writes=[pst], pe_accum=True)
            ot = fs[2]
            P.op("scalar", I("activation", out=ot[:, :], in_=pst[:, :], func=AF.Copy), reads=[pst], writes=[ot])
            toks.append(P.dma(sq, I("dma_start", out=outv[:, :, kc * 128:(kc + 1) * 128], in_=ot[:, :].rearrange("p (s f) -> p s f", f=128)), reads=[ot], writes=[out_t]))
        P.wait_all(sq, toks)
    if ext_out:
        P.wait_all(sq, [tk for t in ext_out for tk in t.ws.items()])
    P.emit()
    st.close()
    return nc, in_names, [t.name for t in ext_out]


def _bf(a):
    return np.ascontiguousarray(np.asarray(a, dtype=np.float32)).astype(ml_dtypes.bfloat16)


def _split3(v):
    v = np.asarray(v, dtype=np.float64)
    p1 = v.astype(np.float32).astype(ml_dtypes.bfloat16)
    r = v - p1.astype(np.float64)
    p2 = r.astype(np.float32).astype(ml_dtypes.bfloat16)
    r = r - p2.astype(np.float64)
    p3 = r.astype(np.float32).astype(ml_dtypes.bfloat16)
    return p1, p2, p3


def _const_tables(c):
    j = np.arange(128)[:, None]
    i = np.arange(128)[None, :]
    caus = np.where(j <= i, BIG, 0.0).astype(np.float32)
    prev = np.where(j > i, BIG, 0.0).astype(np.float32)
    maskB = np.zeros((128, 8, 128), np.float32)
    for m in range(8):
        maskB[:, m, :] = BIG if m < c else (caus if m == c else 0.0)
    maskA = np.zeros((128, 9, 128), np.float32)
    for m in range(9):
        if m == c:
            maskA[:, m, :] = prev
        elif m == c + 1:
            maskA[:, m, :] = caus
    s_ = np.arange(8)[:, None]
    t_ = np.arange(128)[None, :]
    blk_q = (8 * s_ + c) * np.ones((1, 128))
    loc = (t_ * np.ones((8, 1)))
    blk_q = blk_q.reshape(-1); loc_q = loc.reshape(-1)
    r_ = np.arange(8)[:, None, None]
    blk_k = (8 * s_[None] + r_) * np.ones((1, 1, 128))
    blk_k = blk_k.reshape(8, 1024); loc_k = np.tile(loc.reshape(1, -1), (8, 1))
    slA = np.array([2.0 ** (-8.0 * (h + 1) / 16) for h in range(16)], np.float64)
    qaugA = np.zeros((16, 9, 1024), ml_dtypes.bfloat16)
    for h in range(16):
        v = -slA[h] * (128.0 * blk_q + loc_q)
        p = _split3(v)
        sp = _split3(np.array([slA[h]]))
        for k in range(3):
            qaugA[h, k] = p[k]
            qaugA[h, 3 + k] = sp[k][0]
            qaugA[h, 6 + k] = sp[k][0]
    kaugA = np.zeros((9, 8, 1024), ml_dtypes.bfloat16)
    kaugA[0:3] = 1.0
    kaugA[3:6] = (128.0 * blk_k)[None].astype(np.float32)
    kaugA[6:9] = loc_k[None].astype(np.float32)
    slB = np.array([2.0 ** (-8.0 * (h + 1) / 8) for h in range(8)], np.float64)
    qaugB = np.zeros((8, 4, 1024), ml_dtypes.bfloat16)
    kaugB = np.zeros((8, 4, 8, 1024), ml_dtypes.bfloat16)
    for h in range(8):
        qaugB[h, 0] = (-slB[h] * 128.0 * blk_q).astype(np.float32)
        qaugB[h, 1] = (-slB[h] * loc_q).astype(np.float32)
        qaugB[h, 2] = 1.0
        qaugB[h, 3] = 1.0
        kaugB[h, 0] = 1.0
        kaugB[h, 1] = 1.0
        kaugB[h, 2] = (slB[h] * 128.0 * blk_k).astype(np.float32)
        kaugB[h, 3] = (slB[h] * loc_k).astype(np.float32)
    sel = np.zeros((128, 8), np.float32); sel[:, c] = 1.0
    rstm = np.ones((32, 1024), np.float32); rstm[:, 0::128] = 0.0
    return dict(maskB=_bf(maskB), maskA=_bf(maskA), qaugA=qaugA, kaugA=kaugA, qaugB=qaugB, kaugB=kaugB, sel=sel,
                ones3=np.ones((3, 8, 1024), ml_dtypes.bfloat16), rstm=rstm, ident=np.eye(128, dtype=np.float32))


_NC_CACHE = {}
FUSED = False


def _get(stage):
    if stage not in _NC_CACHE:
        _NC_CACHE[stage] = build(stage)
    return _NC_CACHE[stage]


def _launch(stage, maps):
    nc, names, outs = _get(stage)
    in_maps = [{k: m[k] for k in names} for m in maps]
    res = run_bass_kernel_spmd(nc, in_maps, core_ids=list(range(NCORES)))
    return res.results


def kernel(x, p, norm_g, w_in_ab, w_out_ab, attn_sinks, diff_lambda, diff_subln_g,
           w_in_c, w_out_c, forget_bias, ple_proj, ple_gate, ple_norm_g, final_norm_g):
    f32 = lambda a: np.ascontiguousarray(np.asarray(a, dtype=np.float32))
    x = f32(x)[0]; p = f32(p)[:, 0]
    xb = x.reshape(8, 8, 128, D)
    pb = p.reshape(2, 8, 8, 128, 256)
    gvec = np.stack([f32(norm_g)[0], f32(norm_g)[1], f32(ple_norm_g)[0], f32(ple_norm_g)[1], f32(final_norm_g)], 0)
    gvec = np.ascontiguousarray(gvec.reshape(5, KC, 128).transpose(0, 2, 1))
    shared = dict(
        w_in_ab=f32(w_in_ab)[0], w_out_ab=f32(w_out_ab)[0], w_in_c=f32(w_in_c)[0], w_out_c=f32(w_out_c)[0],
        ple_proj=f32(ple_proj), ple_gate=f32(ple_gate), gvec=gvec,
        sinks=np.ascontiguousarray(np.tile(f32(attn_sinks).reshape(1, 16), (128, 1))), lam=np.ascontiguousarray(np.tile(f32(diff_lambda).reshape(1, 256), (128, 1))),
        subg=f32(diff_subln_g).reshape(128, 1), fbias=f32(forget_bias).reshape(32, 1))
    maps = []
    for c in range(NCORES):
        m = dict(shared)
        m["x"] = np.ascontiguousarray(xb[:, c].reshape(NT, D))
        m["p"] = np.ascontiguousarray(pb[:, :, c].reshape(2, NT, 256))
        m.update(_const_tables(c))
        maps.append(m)
    if FUSED:
        res = _launch(0, maps)
    else:
        cat = lambda rs, k: np.ascontiguousarray(np.concatenate([np.asarray(r[k]) for r in rs], axis=0))
        r1 = _launch(1, maps)
        for c in range(NCORES):
            for k in ("qT0", "sg0", "xsav1"):
                maps[c][k] = np.asarray(r1[c][k])
        for k in ("kt0", "v0b", "v0a"):
            g = cat(r1, k + "_loc")
            for c in range(NCORES):
                maps[c][k + "_all"] = g
        r2 = _launch(2, maps)
        for c in range(NCORES):
            for k in ("qT1", "sg1", "xsav2", "w1_loc"):
                maps[c][k] = np.asarray(r2[c][k])
        for k in ("kt1", "v1", "w1"):
            g = cat(r2, k + "_loc")
            for c in range(NCORES):
                maps[c][k + "_all"] = g
        res = _launch(3, maps)
    out = np.zeros((8, 8, 128, D), np.float32)
    for c in range(NCORES):
        out[:, c] = np.asarray(res[c]["out"], dtype=np.float32).reshape(8, 128, D)
    return out.reshape(1, 8192, D)
```

```python
import math
from contextlib import ExitStack
import numpy as np
import ml_dtypes
import concourse.bass as bass
import concourse.mybir as mybir
from concourse.bass_utils import run_bass_kernel_spmd

F32 = mybir.dt.float32
BF16 = mybir.dt.bfloat16
AF = mybir.ActivationFunctionType
ALU = mybir.AluOpType
ENGS = ["sync", "tensor", "vector", "scalar", "gpsimd"]
NCORES = 8
NT = 1024
D = 2048
KC = 16
EPS = 1e-6
BIG = 3.0e38


class T:
    __slots__ = ("name", "h", "ws", "rs")

    def __init__(self, name, h):
        self.name = name
        self.h = h
        self.ws = {}
        self.rs = {}

    def __getitem__(self, idx):
        return self.h[idx]


def I(name, *a, **k):
    return lambda e: getattr(e, name)(*a, **k)


class Prog:
    N_DMA_SEMS = 48

    def __init__(self, nc):
        self.nc = nc
        self.ops = {e: [] for e in ENGS}
        self.cnt = {e: 0 for e in ENGS}
        self.dma_rr = 0
        self.dma_tot = [0] * self.N_DMA_SEMS
        self.seen = {e: {} for e in ENGS}

    def _need(self, eng, key, val, waits):
        if self.seen[eng].get(key, 0) >= val:
            return
        if waits.get(key, 0) < val:
            waits[key] = val

    def _deps(self, eng, reads, writes, pe_accum):
        waits = {}
        for t in reads:
            for k, v in t.ws.items():
                self._need(eng, k, v, waits)
        for t in writes:
            for k, v in t.ws.items():
                if pe_accum and k == ("e", "tensor"):
                    continue
                self._need(eng, k, v, waits)
            for k, v in t.rs.items():
                if k == ("e", eng):
                    continue
                self._need(eng, k, v, waits)
        for k, v in waits.items():
            self.seen[eng][k] = v
        return waits

    def _mark(self, key, val, reads, writes):
        for t in reads:
            if t.rs.get(key, 0) < val:
                t.rs[key] = val
        for t in writes:
            t.ws = {key: val}
            t.rs = {}

    def op(self, eng, fn, reads=(), writes=(), pe_accum=False):
        waits = self._deps(eng, reads, writes, pe_accum)
        self.cnt[eng] += 1
        self.ops[eng].append((fn, waits, "c", None))
        self._mark(("e", eng), self.cnt[eng], reads, writes)
        return (("e", eng), self.cnt[eng])

    def dma(self, eng, fn, reads=(), writes=()):
        waits = self._deps(eng, reads, writes, False)
        k = self.dma_rr
        self.dma_rr = (self.dma_rr + 1) % self.N_DMA_SEMS
        key = ("d", k)
        if self.dma_tot[k] > 0 and self.seen[eng].get(key, 0) < self.dma_tot[k]:
            waits[key] = self.dma_tot[k]
            self.seen[eng][key] = self.dma_tot[k]
        self.dma_tot[k] += 16
        self.ops[eng].append((fn, waits, "d", k))
        self._mark(key, self.dma_tot[k], reads, writes)
        return (key, self.dma_tot[k])

    def wait_all(self, eng, toks):
        waits = {}
        for key, val in toks:
            self._need(eng, key, val, waits)
        for k, v in waits.items():
            self.seen[eng][k] = v
        self.ops[eng].append((None, waits, "w", None))

    def emit(self):
        nc = self.nc
        with ExitStack() as st:
            esem = {e: st.enter_context(nc.semaphore("es_" + e)) for e in ENGS}
            dsem = [st.enter_context(nc.semaphore("ds_%d" % i)) for i in range(self.N_DMA_SEMS)]
            block = st.enter_context(nc.Block())

            def mk(ename):
                def body(eng):
                    for fn, waits, kind, k in self.ops[ename]:
                        for key, val in waits.items():
                            sem = esem[key[1]] if key[0] == "e" else dsem[key[1]]
                            eng.wait_ge(sem, val)
                        if kind == "c":
                            fn(eng).then_inc(esem[ename], 1)
                        elif kind == "d":
                            fn(eng).then_inc(dsem[k], 16)
                return body

            block.sync(mk("sync"))
            block.tensor(mk("tensor"))
            block.vector(mk("vector"))
            block.scalar(mk("scalar"))
            block.gpsimd(mk("gpsimd"))


def build(stage=0):
    nc = bass.Bass("TRN2", target_bir_lowering=False)
    st = ExitStack()
    P = Prog(nc)

    in_names = []

    def ein(name, shape, dt=F32):
        in_names.append(name)
        return T(name, nc.dram_tensor(name, list(shape), dt, kind="ExternalInput").ap())

    ext_out = []

    def dint(name, shape, dt=BF16, prod=0, cons=0):
        if stage == 0 or prod == cons:
            h = nc.dram_tensor(name, list(shape), dt)
            return T(name, h[tuple(slice(None) for _ in shape)])
        if stage == prod:
            t = T(name, nc.dram_tensor(name, list(shape), dt, kind="ExternalOutput").ap())
            ext_out.append(t)
            return t
        if stage == cons:
            in_names.append(name)
            return T(name, nc.dram_tensor(name, list(shape), dt, kind="ExternalInput").ap())
        h = nc.dram_tensor(name, list(shape), dt)
        return T(name, h[tuple(slice(None) for _ in shape)])

    def sb(name, shape, dt):
        return T(name, st.enter_context(nc.sbuf_tensor("sb_" + name, list(shape), dt)))

    def ps(name):
        return T(name, st.enter_context(nc.psum_tensor(name, [128, 1024], F32)))

    x_in = ein("x", [NT, D])
    p_in = ein("p", [2, NT, 256])
    w_in_ab = ein("w_in_ab", [D, 6400])
    w_out_ab = ein("w_out_ab", [D, D])
    w_in_c = ein("w_in_c", [D, 8224])
    w_out_c = ein("w_out_c", [D, D])
    ple_proj = ein("ple_proj", [2, 256, D])
    ple_gate = ein("ple_gate", [2, D, D])
    gvec_in = ein("gvec", [5, 128, KC])
    sinks_in = ein("sinks", [128, 16])
    lam_in = ein("lam", [128, 256])
    subg_in = ein("subg", [128, 1])
    fb_in = ein("fbias", [32, 1])
    ident_in = ein("ident", [128, 128])
    sel_in = ein("sel", [128, 8])
    maskB_in = ein("maskB", [128, 8, 128], BF16)
    maskA_in = ein("maskA", [128, 9, 128], BF16)
    qaugA_in = ein("qaugA", [16, 9, NT], BF16)
    kaugA_in = ein("kaugA", [9, 8, NT], BF16)
    qaugB_in = ein("qaugB", [8, 4, NT], BF16)
    kaugB_in = ein("kaugB", [8, 4, 8, NT], BF16)
    ones3_in = ein("ones3", [3, 8, NT], BF16)
    rst_in = ein("rstm", [32, NT])
    out_t = T("out", nc.dram_tensor("out", [NT, D], F32, kind="ExternalOutput").ap()) if stage in (0, 3) else None

    qT0 = dint("qT0", [2048, NT], BF16, 1, 2); sg0 = dint("sg0", [2048, NT], BF16, 1, 2)
    kt0_loc = dint("kt0_loc", [1152, NT], BF16, 1, 9); kt0_all = dint("kt0_all", [8 * 1152, NT], BF16, 9, 2)
    v0b_loc = dint("v0b_loc", [1024, 1024], BF16, 1, 9); v0b_all = dint("v0b_all", [8 * 1024, 1024], BF16, 9, 2)
    v0a_loc = dint("v0a_loc", [256, 512], BF16, 1, 9); v0a_all = dint("v0a_all", [8 * 256, 512], BF16, 9, 2)
    qT1 = dint("qT1", [2048, NT], BF16, 2, 3); sg1 = dint("sg1", [2048, NT], BF16, 2, 3)
    kt1_loc = dint("kt1_loc", [2048, NT], BF16, 2, 9); kt1_all = dint("kt1_all", [8 * 2048, NT], BF16, 9, 3)
    v1_loc = dint("v1_loc", [4096, 512], BF16, 2, 9); v1_all = dint("v1_all", [8 * 4096, 512], BF16, 9, 3)
    w1_loc = dint("w1_loc", [32, NT], F32, 2, 3); w1_all = dint("w1_all", [8 * 32, NT], F32, 9, 3)
    xsav1 = dint("xsav1", [128, KC * NT], F32, 1, 2); xsav2 = dint("xsav2", [128, KC * NT], F32, 2, 3)
    qaug1 = dint("qaug1", [3, 32, NT]); kaug1 = dint("kaug1", [3, 32, 8, NT])

    xT = sb("xT", [128, KC, NT], F32)
    hT = sb("hT", [128, KC, NT], BF16)
    wts = [sb("wt%d" % i, [128, KC, 128], BF16) for i in range(3)]
    zts = [sb("zt%d" % i, [128, NT], BF16) for i in range(2)]
    vts = [sb("vt%d" % i, [128, 8, 128], BF16) for i in range(1)]
    ktr = [sb("kt%d" % i, [128, 2048], BF16) for i in range(3)]
    vtr = [sb("vr%d" % i, [128, 16, 128], BF16) for i in range(3)]
    qts = [sb("qt%d" % i, [128, 2, NT], BF16) for i in range(2)]
    gts = [sb("gt%d" % i, [128, NT], BF16) for i in range(2)]
    pts = [sb("pt%d" % i, [128, NT], BF16) for i in range(2)]
    fs = [sb("fs%d" % i, [128, NT], F32) for i in range(4)]
    bs_ = [sb("bs%d" % i, [128, NT], BF16) for i in range(2)]
    pT = qts[0]
    ppw = [sb("ppw%d" % i, [128, 2, 128], BF16) for i in range(2)]
    gv = sb("gv", [128, 5, KC], F32)
    ident = sb("ident", [128, 128], F32)
    ones_bf = sb("ones_bf", [128, 128], BF16)
    maskB = sb("maskB", [128, 8, 128], BF16)
    maskA = sb("maskA", [128, 9, 128], BF16)
    sm = sb("sm", [128, 768], F32)
    PS = [ps("ps%d" % i) for i in range(4)]

    C_ES = 0
    C_LAM = 16
    C_NLAM = 280
    C_SUBG = 281
    C_NFB = 282
    C_SEL = 284
    C_T = 296

    sq = "sync"

    P.dma(sq, I("dma_start", out=gv[:, :, :], in_=gvec_in[:, :, :].rearrange("g p k -> p g k")), writes=[gv])
    P.dma(sq, I("dma_start", out=ident[:, :], in_=ident_in[:, :]), writes=[ident])
    P.dma(sq, I("dma_start", out=maskB[:, :, :], in_=maskB_in[:, :, :]), writes=[maskB])
    P.dma(sq, I("dma_start", out=maskA[:, :, :], in_=maskA_in[:, :, :]), writes=[maskA])
    P.op("vector", I("memset", ones_bf[:, :], 1.0), writes=[ones_bf])
    P.dma(sq, I("dma_start", out=sm[:, C_ES:C_ES + 16], in_=sinks_in[:, :]), writes=[sm])
    P.dma(sq, I("dma_start", out=sm[:, C_LAM:C_LAM + 256], in_=lam_in[:, :]), writes=[sm])
    P.dma(sq, I("dma_start", out=sm[:, C_SUBG:C_SUBG + 1], in_=subg_in[:, :]), writes=[sm])
    P.dma(sq, I("dma_start", out=sm[0:32, C_NFB:C_NFB + 1], in_=fb_in[:, :]), writes=[sm])
    P.dma(sq, I("dma_start", out=sm[:, C_SEL:C_SEL + 8], in_=sel_in[:, :]), writes=[sm])
    P.op("scalar", I("activation", out=sm[:, C_ES:C_ES + 16], in_=sm[:, C_ES:C_ES + 16], func=AF.Exp), reads=[sm], writes=[sm])
    lam_init = 0.8 - 0.6 * math.exp(-0.3 * 0)
    P.op("vector", I("tensor_tensor", out=sm[:, C_T:C_T + 64], in0=sm[:, C_LAM:C_LAM + 64], in1=sm[:, C_LAM + 64:C_LAM + 128], op=ALU.mult), reads=[sm], writes=[sm])
    P.op("vector", I("tensor_reduce", out=sm[:, C_T + 128:C_T + 129], in_=sm[:, C_T:C_T + 64], axis=mybir.AxisListType.X, op=ALU.add), reads=[sm], writes=[sm])
    P.op("vector", I("tensor_tensor", out=sm[:, C_T:C_T + 64], in0=sm[:, C_LAM + 128:C_LAM + 192], in1=sm[:, C_LAM + 192:C_LAM + 256], op=ALU.mult), reads=[sm], writes=[sm])
    P.op("vector", I("tensor_reduce", out=sm[:, C_T + 129:C_T + 130], in_=sm[:, C_T:C_T + 64], axis=mybir.AxisListType.X, op=ALU.add), reads=[sm], writes=[sm])
    P.op("scalar", I("activation", out=sm[:, C_T + 128:C_T + 130], in_=sm[:, C_T + 128:C_T + 130], func=AF.Exp), reads=[sm], writes=[sm])
    P.op("vector", I("scalar_tensor_tensor", out=sm[:, C_NLAM:C_NLAM + 1], in0=sm[:, C_T + 129:C_T + 130], scalar=-lam_init,
                     in1=sm[:, C_T + 128:C_T + 129], op0=ALU.add, op1=ALU.subtract), reads=[sm], writes=[sm])
    P.op("vector", I("tensor_scalar", out=sm[:, C_SUBG:C_SUBG + 1], in0=sm[:, C_SUBG:C_SUBG + 1], scalar1=1.0 - lam_init, scalar2=None, op0=ALU.mult), reads=[sm], writes=[sm])
    P.op("vector", I("tensor_scalar", out=sm[0:32, C_NFB:C_NFB + 1], in0=sm[0:32, C_NFB:C_NFB + 1], scalar1=-1.0, scalar2=None, op0=ALU.mult), reads=[sm], writes=[sm])

    def save_x(dst):
        P.dma(sq, I("dma_start", out=dst[:, :], in_=xT[:, :, :].rearrange("p k t -> p (k t)")), reads=[xT], writes=[dst])

    if stage == 2:
        P.dma(sq, I("dma_start", out=xT[:, :, :].rearrange("p k t -> p (k t)"), in_=xsav1[:, :]), reads=[xsav1], writes=[xT])
    if stage == 3:
        P.dma(sq, I("dma_start", out=xT[:, :, :].rearrange("p k t -> p (k t)"), in_=xsav2[:, :]), reads=[xsav2], writes=[xT])

    if stage in (0, 1):
        for s in range(8):
            for hf in range(2):
                f = fs[hf]
                P.dma(sq, I("dma_start", out=f[:, :], in_=x_in[s * 128:(s + 1) * 128, hf * 1024:(hf + 1) * 1024]), writes=[f])
                pst = PS[(s * 2 + hf) % 4]
                for j in range(8):
                    P.op("tensor", I("transpose", out=pst[:, j * 128:(j + 1) * 128], in_=f[:, j * 128:(j + 1) * 128], identity=ident[:, :]),
                         reads=[f, ident], writes=[pst], pe_accum=True)
                P.op("vector", I("tensor_copy", out=xT[:, hf * 8:(hf + 1) * 8, s * 128:(s + 1) * 128],
                                 in_=pst[:, :].rearrange("p (j t) -> p j t", t=128)), reads=[pst], writes=[xT])

    def rstd_bc(dst):
        pst = PS[0]
        for kc in range(KC):
            b = bs_[kc % 2]
            P.op("scalar", I("activation", out=b[:, :], in_=xT[:, kc, :], func=AF.Square), reads=[xT], writes=[b])
            for hf in range(2):
                P.op("tensor", I("matmul", pst[:, hf * 512:(hf + 1) * 512], lhsT=ones_bf[:, :], rhs=b[:, hf * 512:(hf + 1) * 512],
                                 start=(kc == 0), stop=(kc == KC - 1)), reads=[ones_bf, b], writes=[pst], pe_accum=True)
        P.op("scalar", I("activation", out=dst[:, :], in_=pst[:, :], func=AF.Sqrt, bias=EPS_AP(), scale=1.0 / D), reads=[pst, sm], writes=[dst])
        P.op("vector", I("reciprocal", out=dst[:, :], in_=dst[:, :]), reads=[dst], writes=[dst])

    def EPS_AP():
        return sm[:, C_T + 140:C_T + 141]

    P.op("vector", I("memset", sm[:, C_T + 140:C_T + 141], EPS), writes=[sm])
    P.op("vector", I("memset", sm[:, C_T + 141:C_T + 142], 1.0), writes=[sm])

    def norm_to_hT(gidx):
        r = fs[3]
        rstd_bc(r)
        for kc in range(KC):
            P.op("vector", I("scalar_tensor_tensor", out=hT[:, kc, :], in0=xT[:, kc, :], scalar=gv[:, gidx, kc:kc + 1], in1=r[:, :],
                             op0=ALU.mult, op1=ALU.mult), reads=[xT, gv, r], writes=[hT])

    wq = "gpsimd"
    wstate = {"i": 0}

    def load_w(w_t, col0, n):
        wt = wts[wstate["i"] % 3]
        wstate["i"] += 1
        P.dma(wq, I("dma_start", out=wt[:, :, 0:n], in_=w_t.h.rearrange("(kc p) c -> p kc c", p=128)[:, :, col0:col0 + n]), writes=[wt])
        return wt

    def proj_groups(groups, w_t, rhsT, handler):
        pend = []
        for gi in range(min(2, len(groups))):
            pend.append(load_w(w_t, groups[gi][0], groups[gi][1]))
        for gi, (col0, n, info) in enumerate(groups):
            wt = pend.pop(0)
            if gi + 2 < len(groups):
                pend.append(load_w(w_t, groups[gi + 2][0], groups[gi + 2][1]))
            pst = PS[gi % 2]
            for hf in range(2):
                for kc in range(KC):
                    P.op("tensor", I("matmul", pst[0:n, hf * 512:(hf + 1) * 512], lhsT=wt[:, kc, 0:n], rhs=rhsT[:, kc, hf * 512:(hf + 1) * 512],
                                     start=(kc == 0), stop=(kc == KC - 1)), reads=[wt, rhsT], writes=[pst], pe_accum=True)
            handler(gi, pst, n, info)

    zstate = {"i": 0}

    def evac_store(pst, n, func, scale, dst_t, row0):
        zt = zts[zstate["i"] % 2]
        zstate["i"] += 1
        P.op("scalar", I("activation", out=zt[0:n, :], in_=pst[0:n, :], func=func, scale=scale), reads=[pst], writes=[zt])
        P.dma(sq, I("dma_start", out=dst_t[row0:row0 + n, :], in_=zt[0:n, :]), reads=[zt], writes=[dst_t])

    def evac_v(pst, dst_ap_fn, dst_t):
        f = fs[0]
        P.op("scalar", I("activation", out=f[:, :], in_=pst[:, :], func=AF.Copy), reads=[pst], writes=[f])
        p2 = PS[2]
        for s in range(8):
            P.op("tensor", I("transpose", out=p2[:, s * 128:(s + 1) * 128], in_=f[:, s * 128:(s + 1) * 128], identity=ident[:, :]),
                 reads=[f, ident], writes=[p2], pe_accum=True)
        vt = vts[0]
        P.op("vector", I("tensor_copy", out=vt[:, :, :], in_=p2[:, :].rearrange("p (s f) -> p s f", f=128)), reads=[p2], writes=[vt])
        for (dst_ap, src_ap) in dst_ap_fn(vt):
            P.dma(sq, I("dma_start", out=dst_ap, in_=src_ap), reads=[vt], writes=[dst_t])

    def allgather(loc, allt):
        P.op("gpsimd", I("collective_compute", "AllGather", ALU.bypass, replica_groups=[list(range(NCORES))],
                         ins=[loc.h], outs=[allt.h]), reads=[loc], writes=[allt])

    ring = {"k": 0, "p": 0, "s": 0}

    def next_kv():
        i = ring["k"] % 3
        ring["k"] += 1
        return ktr[i], vtr[i]

    def s_exp(kt, kslot, K, q_ap, q_t, n):
        S = PS[ring["s"] % 2]
        ring["s"] += 1
        return S

    def attn_full(K, qt, qrows_loader, kv_loader, vmode, O, Sm):
        pend = None
        for qc in range(4):
            kt, vt = next_kv()
            kv_loader(qc, kt, vt)
            for J in range(16 * qc, 16 * qc + 16):
                slot = (J % 8) * 2 + (J // 8 - 2 * qc)
                lo = (J // 8) * 128
                pieces = []
                if lo < 512:
                    pieces.append((lo, 512))
                pieces.append((max(lo, 512), 1024))
                S = PS[ring["s"] % 2]
                ring["s"] += 1
                for (a, b) in pieces:
                    P.op("tensor", I("matmul", S[:, a:b], lhsT=kt[0:K, slot * 128:(slot + 1) * 128], rhs=qt[0:K, 0, a:b], start=True, stop=True),
                         reads=[kt, qt], writes=[S], pe_accum=True)
                pt = pts[ring["p"] % 2]
                ring["p"] += 1
                P.op("scalar", I("activation", out=pt[:, lo:1024], in_=S[:, lo:1024], func=AF.Exp), reads=[S], writes=[pt])
                P.op("vector", I("tensor_tensor", out=pt[:, lo:lo + 128], in0=pt[:, lo:lo + 128], in1=maskB[:, J % 8, :], op=ALU.min),
                     reads=[pt, maskB], writes=[pt])
                if pend is not None:
                    pend()

                def pv(J=J, slot=slot, pieces=pieces, pt=pt, vt=vt):
                    for (a, b) in pieces:
                        P.op("tensor", I("matmul", O[:, a:b], lhsT=vt[:, slot, :], rhs=pt[:, a:b], start=(J == 0), stop=(J == 63)),
                             reads=[vt, pt], writes=[O], pe_accum=True)
                        if vmode == "b":
                            P.op("tensor", I("matmul", Sm[:, a:b], lhsT=ones_bf[:, :], rhs=pt[:, a:b], start=(J == 0), stop=(J == 63)),
                                 reads=[ones_bf, pt], writes=[Sm], pe_accum=True)
                pend = pv
        pend()

    def set_v_ones():
        for vt in vtr:
            P.op("vector", I("memset", vt[:, :, 64:128], 1.0), writes=[vt])

    if stage in (0, 1):
        norm_to_hT(0)
    g0 = []
    for j in range(8):
        g0.append((0 + j * 128, 128, ("q", qT0, j * 128)))
    g0.append((1024, 128, ("k", kt0_loc, 1024)))
    g0.append((1152, 128, ("va",)))
    for j in range(8):
        g0.append((1280 + j * 128, 128, ("g", sg0, j * 128)))
    for j in range(8):
        g0.append((2304 + j * 128, 128, ("q", qT0, 1024 + j * 128)))
    for j in range(8):
        g0.append((3328 + j * 128, 128, ("k", kt0_loc, j * 128)))
    for j in range(8):
        g0.append((4352 + j * 128, 128, ("vb", j)))
    for j in range(8):
        g0.append((5376 + j * 128, 128, ("g", sg0, 1024 + j * 128)))

    def h0(gi, pst, n, info):
        k = info[0]
        if k == "q":
            evac_store(pst, n, AF.Copy, 0.125, info[1], info[2])
        elif k == "k":
            evac_store(pst, n, AF.Copy, 1.0, info[1], info[2])
        elif k == "g":
            evac_store(pst, n, AF.Silu, 1.0, info[1], info[2])
        elif k == "vb":
            hd = info[1]
            evac_v(pst, lambda vt: [(v0b_loc[hd * 128:(hd + 1) * 128, :].rearrange("p (s d) -> p s d", d=128), vt[:, :, :])], v0b_loc)
        elif k == "va":
            evac_v(pst, lambda vt: [(v0a_loc[g * 128:(g + 1) * 128, :].rearrange("p (s d) -> p s d", d=64), vt[:, :, g * 64:(g + 1) * 64]) for g in range(2)], v0a_loc)

    if stage in (0, 1):
        proj_groups(g0, w_in_ab, hT, h0)
        if stage == 1:
            save_x(xsav1)
    if stage == 0:
        allgather(kt0_loc, kt0_all)
    if stage == 0:
        allgather(v0b_loc, v0b_all)
    if stage == 0:
        allgather(v0a_loc, v0a_all)

    kt0v = kt0_all.h.rearrange("(r m) t -> m r t", r=8)
    v0bv = v0b_all.h.rearrange("(r h p) (s d) -> h p r s d", r=8, h=8, d=128)
    v0av = v0a_all.h.rearrange("(r g p) (s d) -> g p r s d", r=8, g=2, d=64)
    yT = hT

    if stage in (0, 2):
        set_v_ones()
        KA = 73
        ai = 0
        for g in range(2):
            for dd in range(4):
                h0_ = g * 8 + dd * 2
                qt = qts[(g * 4 + dd) % 2]
                P.dma(sq, I("dma_start", out=qt[0:64, :, :], in_=qT0[h0_ * 64:(h0_ + 2) * 64, :].rearrange("(i d) t -> d i t", d=64)), reads=[qT0], writes=[qt])
                P.dma(sq, I("dma_start", out=qt[64:73, :, :], in_=qaugA_in[h0_:h0_ + 2, :, :].rearrange("i r t -> r i t")), writes=[qt])
                gt = gts[(g * 4 + dd) % 2]
                P.dma(sq, I("dma_start", out=gt[:, :], in_=sg0[h0_ * 64:(h0_ + 2) * 64, :]), reads=[sg0], writes=[gt])
                for s in range(8):
                    kt, vt = next_kv()
                    rows = slice(1024 + g * 64, 1024 + (g + 1) * 64)
                    if s > 0:
                        P.dma(sq, I("dma_start", out=kt[0:64, 0:128], in_=kt0v[rows, 7, (s - 1) * 128:s * 128]), reads=[kt0_all], writes=[kt])
                        P.dma(sq, I("dma_start", out=kt[64:73, 0:128], in_=kaugA_in[:, 7, (s - 1) * 128:s * 128]), writes=[kt])
                        P.dma(sq, I("dma_start", out=vt[:, 0, 0:64], in_=v0av[g, :, 7, s - 1, :]), reads=[v0a_all], writes=[vt])
                    P.dma(sq, I("dma_start", out=kt[0:64, 128:1152].rearrange("m (r t) -> m r t", t=128), in_=kt0v[rows, :, s * 128:(s + 1) * 128]), reads=[kt0_all], writes=[kt])
                    P.dma(sq, I("dma_start", out=kt[64:73, 128:1152].rearrange("m (r t) -> m r t", t=128), in_=kaugA_in[:, :, s * 128:(s + 1) * 128]), writes=[kt])
                    P.dma(sq, I("dma_start", out=vt[:, 1:9, 0:64], in_=v0av[g, :, :, s, :]), reads=[v0a_all], writes=[vt])
                    O = PS[2 + (ai % 2)]
                    first = 1 if s == 0 else 0
                    pend = None
                    for idx in range(first, 9):
                        S = PS[ring["s"] % 2]
                        ring["s"] += 1
                        P.op("tensor", I("matmul", S[:, 0:256], lhsT=kt[0:KA, idx * 128:(idx + 1) * 128], rhs=qt[0:KA, :, s * 128:(s + 1) * 128], start=True, stop=True),
                             reads=[kt, qt], writes=[S], pe_accum=True)
                        pt = pts[ring["p"] % 2]
                        ring["p"] += 1
                        P.op("scalar", I("activation", out=pt[:, 0:256], in_=S[:, 0:256], func=AF.Exp), reads=[S], writes=[pt])
                        for i in range(2):
                            P.op("vector", I("tensor_tensor", out=pt[:, i * 128:(i + 1) * 128], in0=pt[:, i * 128:(i + 1) * 128], in1=maskA[:, idx, :], op=ALU.min),
                                 reads=[pt, maskA], writes=[pt])
                        if pend is not None:
                            pend()

                        def pv(idx=idx, pt=pt, vt=vt, O=O, first=first):
                            P.op("tensor", I("matmul", O[:, 0:256], lhsT=vt[:, idx, :], rhs=pt[:, 0:256], start=(idx == first), stop=(idx == 8)),
                                 reads=[vt, pt], writes=[O], pe_accum=True)
                        pend = pv
                    pend()
                    rc = fs[0]
                    for i in range(2):
                        h = h0_ + i
                        pb = i * 64
                        P.op("vector", I("tensor_scalar", out=rc[pb:pb + 64, 0:128], in0=O[64:128, i * 128:(i + 1) * 128],
                                         scalar1=sm[64:128, C_ES + h:C_ES + h + 1], scalar2=None, op0=ALU.add), reads=[O, sm], writes=[rc])
                        P.op("vector", I("reciprocal", out=rc[pb:pb + 64, 0:128], in_=rc[pb:pb + 64, 0:128]), reads=[rc], writes=[rc])
                        P.op("vector", I("tensor_tensor", out=rc[pb:pb + 64, 0:128], in0=rc[pb:pb + 64, 0:128],
                                         in1=gt[pb:pb + 64, s * 128:(s + 1) * 128], op=ALU.mult), reads=[rc, gt], writes=[rc])
                        hb = (h % 2) * 64
                        P.op("vector", I("tensor_tensor", out=yT[hb:hb + 64, h // 2, s * 128:(s + 1) * 128], in0=O[0:64, i * 128:(i + 1) * 128],
                                         in1=rc[pb:pb + 64, 0:128], op=ALU.mult), reads=[O, rc], writes=[yT])
                    ai += 1

        KB = 68
        for hd in range(8):
            gt = gts[hd % 2]
            P.dma(sq, I("dma_start", out=gt[:, :], in_=sg0[1024 + hd * 128:1024 + (hd + 1) * 128, :]), reads=[sg0], writes=[gt])
            for c in range(2):
                u = hd * 2 + c
                qt = qts[u % 2]
                P.dma(sq, I("dma_start", out=qt[0:64, 0, :], in_=qT0[1024 + u * 64:1024 + (u + 1) * 64, :]), reads=[qT0], writes=[qt])
                P.dma(sq, I("dma_start", out=qt[64:68, 0, :], in_=qaugB_in[hd, :, :]), writes=[qt])

                def kvl(qc, kt, vt, u=u, hd=hd):
                    P.dma(sq, I("dma_start", out=kt[0:64, :].rearrange("m (r t) -> m r t", t=256), in_=kt0v[u * 64:(u + 1) * 64, :, qc * 256:(qc + 1) * 256]), reads=[kt0_all], writes=[kt])
                    P.dma(sq, I("dma_start", out=kt[64:68, :].rearrange("m (r t) -> m r t", t=256), in_=kaugB_in[hd, :, :, qc * 256:(qc + 1) * 256]), writes=[kt])
                    for r in range(8):
                        P.dma(sq, I("dma_start", out=vt[:, 2 * r:2 * r + 2, :], in_=v0bv[hd, :, r, 2 * qc:2 * qc + 2, :]), reads=[v0b_all], writes=[vt])

                O, Sm = PS[2], PS[3]
                attn_full(KB, qt, None, kvl, "b", O, Sm)
                rc = fs[0]
                P.op("vector", I("reciprocal", out=rc[:, :], in_=Sm[:, :]), reads=[Sm], writes=[rc])
                un = fs[1 + c]
                P.op("vector", I("tensor_tensor", out=un[:, :], in0=O[:, :], in1=rc[:, :], op=ALU.mult), reads=[O, rc], writes=[un])
            U = fs[1]
            P.op("vector", I("scalar_tensor_tensor", out=U[:, :], in0=fs[2][:, :], scalar=sm[:, C_NLAM:C_NLAM + 1], in1=fs[1][:, :], op0=ALU.mult, op1=ALU.add),
                 reads=[fs[1], fs[2], sm], writes=[U])
            b = bs_[0]
            P.op("scalar", I("activation", out=b[:, :], in_=U[:, :], func=AF.Square), reads=[U], writes=[b])
            pst = PS[0]
            for hf in range(2):
                P.op("tensor", I("matmul", pst[:, hf * 512:(hf + 1) * 512], lhsT=ones_bf[:, :], rhs=b[:, hf * 512:(hf + 1) * 512], start=True, stop=True),
                     reads=[ones_bf, b], writes=[pst], pe_accum=True)
            r = fs[0]
            P.op("scalar", I("activation", out=r[:, :], in_=pst[:, :], func=AF.Sqrt, bias=EPS_AP(), scale=1.0 / 128), reads=[pst, sm], writes=[r])
            P.op("vector", I("reciprocal", out=r[:, :], in_=r[:, :]), reads=[r], writes=[r])
            P.op("vector", I("scalar_tensor_tensor", out=U[:, :], in0=U[:, :], scalar=sm[:, C_SUBG:C_SUBG + 1], in1=r[:, :], op0=ALU.mult, op1=ALU.mult),
                 reads=[U, sm, r], writes=[U])
            P.op("vector", I("tensor_tensor", out=yT[:, 8 + hd, :], in0=U[:, :], in1=gt[:, :], op=ALU.mult), reads=[U, gt], writes=[yT])

    def phase_R(li, w_out_t):
        groups = [(fc * 128, 128, fc) for fc in range(KC)]

        def hres(gi, pst, n, fc):
            P.op("vector", I("tensor_tensor", out=xT[:, fc, :], in0=pst[:, :], in1=xT[:, fc, :], op=ALU.add), reads=[pst, xT], writes=[xT])

        proj_groups(groups, w_out_t, yT, hres)
        norm_to_hT(2 + li)
        for s0 in range(0, 8, 4):
            f = fs[0]
            P.dma(sq, I("dma_start", out=f[:, :].rearrange("p (s c) -> p s c", c=256), in_=p_in[li, s0 * 128:(s0 + 4) * 128, :].rearrange("(s p) c -> p s c", p=128)), writes=[f])
            pst = PS[2]
            for k2 in range(2):
                for sl in range(4):
                    j = k2 * 4 + sl
                    P.op("tensor", I("transpose", out=pst[:, j * 128:(j + 1) * 128], in_=f[:, sl * 256 + k2 * 128:sl * 256 + (k2 + 1) * 128], identity=ident[:, :]),
                         reads=[f, ident], writes=[pst], pe_accum=True)
            P.op("vector", I("tensor_copy", out=pT[:, :, s0 * 128:(s0 + 4) * 128], in_=pst[:, :].rearrange("p (k t) -> p k t", k=2)), reads=[pst], writes=[pT])
        pwv = ple_proj.h[li].rearrange("(k p) c -> p k c", p=128)
        gw = T("pg%d" % li, ple_gate.h[li])

        def hgate(gi, pst, n, fc):
            pw = ppw[gi % 2]
            P.dma(wq, I("dma_start", out=pw[:, :, :], in_=pwv[:, :, fc * 128:(fc + 1) * 128]), writes=[pw])
            p2 = PS[2 + gi % 2]
            for hf in range(2):
                for k2 in range(2):
                    P.op("tensor", I("matmul", p2[:, hf * 512:(hf + 1) * 512], lhsT=pw[:, k2, :], rhs=pT[:, k2, hf * 512:(hf + 1) * 512], start=(k2 == 0), stop=(k2 == 1)),
                         reads=[pw, pT], writes=[p2], pe_accum=True)
            gs = fs[gi % 2]
            P.op("scalar", I("activation", out=gs[:, :], in_=pst[:, :], func=AF.Sigmoid), reads=[pst], writes=[gs])
            P.op("vector", I("tensor_tensor", out=gs[:, :], in0=p2[:, :], in1=gs[:, :], op=ALU.mult), reads=[p2, gs], writes=[gs])
            P.op("gpsimd", I("tensor_tensor", out=xT[:, fc, :], in0=xT[:, fc, :], in1=gs[:, :], op=ALU.add), reads=[xT, gs], writes=[xT])

        proj_groups(groups, gw, hT, hgate)

    if stage in (0, 2):
        phase_R(0, w_out_ab)

    if True:
        if stage in (0, 2):
            norm_to_hT(1)
        g1 = []
        for j in range(16):
            g1.append((j * 128, 128, ("q", qT1, j * 128)))
        for j in range(16):
            g1.append((2048 + j * 128, 128, ("k", kt1_loc, j * 128)))
        for j in range(16):
            g1.append((4096 + j * 128, 128, ("v", j)))
        g1.append((6144, 32, ("f",)))
        for j in range(16):
            g1.append((6176 + j * 128, 128, ("g", sg1, j * 128)))

        rstm = fs[2]
        P.dma(sq, I("dma_start", out=rstm[0:32, :], in_=rst_in[:, :]), writes=[rstm])

        def h1(gi, pst, n, info):
            k = info[0]
            if k == "q":
                evac_store(pst, n, AF.Copy, 0.125, info[1], info[2])
            elif k == "k":
                evac_store(pst, n, AF.Copy, 1.0, info[1], info[2])
            elif k == "g":
                evac_store(pst, n, AF.Silu, 1.0, info[1], info[2])
            elif k == "v":
                j = info[1]
                evac_v(pst, lambda vt: [(v1_loc[(2 * j + e) * 128:(2 * j + e + 1) * 128, :].rearrange("p (s d) -> p s d", d=64), vt[:, :, e * 64:(e + 1) * 64]) for e in range(2)], v1_loc)
            elif k == "f":
                f = fs[1]
                P.op("scalar", I("activation", out=f[0:32, :], in_=pst[0:32, :], func=AF.Exp, bias=sm[0:32, C_NFB:C_NFB + 1], scale=-1.0), reads=[pst, sm], writes=[f])
                P.op("scalar", I("activation", out=f[0:32, :], in_=f[0:32, :], func=AF.Ln, bias=sm[0:32, C_T + 141:C_T + 142], scale=1.0), reads=[f, sm], writes=[f])
                P.op("vector", I("tensor_scalar", out=f[0:32, :], in0=f[0:32, :], scalar1=-1.0, scalar2=None, op0=ALU.mult), reads=[f], writes=[f])
                P.op("vector", I("tensor_tensor_scan", out=f[0:32, :], data0=rstm[0:32, :], data1=f[0:32, :], initial=0.0, op0=ALU.mult, op1=ALU.add),
                     reads=[f, rstm], writes=[f])
                P.dma(sq, I("dma_start", out=w1_loc[:, :], in_=f[0:32, :]), reads=[f], writes=[w1_loc])

        if stage in (0, 2):
            proj_groups(g1, w_in_c, hT, h1)
            if stage == 2:
                save_x(xsav2)
        if stage == 0:
            allgather(kt1_loc, kt1_all)
        if stage == 0:
            allgather(v1_loc, v1_all)
        if stage == 0:
            allgather(w1_loc, w1_all)

        if stage in (0, 3):
            w1v = w1_all.h.rearrange("(r h) t -> h r t", r=8)
            B1, B2, B3, B4 = 512, 576, 640, 704
            for r in range(8):
                c_ = fs[r % 2]
                P.dma(sq, I("dma_start", out=c_[0:32, :], in_=w1v[:, r, :]), reads=[w1_all], writes=[c_])
                P.op("vector", I("tensor_copy", out=sm[0:32, B1 + r * 8:B1 + r * 8 + 8], in_=c_[0:32, :].rearrange("h (s t) -> h s t", t=128)[:, :, 127]), reads=[c_], writes=[sm])
            P.op("vector", I("tensor_copy", out=sm[0:32, B2:B2 + 64].rearrange("h (s r) -> h s r", r=8), in_=sm[0:32, B1:B1 + 64].rearrange("h (r s) -> h s r", s=8)), reads=[sm], writes=[sm])
            P.op("vector", I("tensor_tensor_scan", out=sm[0:32, B3:B3 + 64], data0=rstm[0:32, 1:65], data1=sm[0:32, B2:B2 + 64], initial=0.0, op0=ALU.mult, op1=ALU.add), reads=[sm, rstm], writes=[sm])
            P.op("vector", I("tensor_tensor", out=sm[0:32, B3:B3 + 64], in0=sm[0:32, B3:B3 + 64], in1=sm[0:32, B2:B2 + 64], op=ALU.subtract), reads=[sm], writes=[sm])
            offs = sm[0:32, B3:B3 + 64].rearrange("h (s r) -> h s r", r=8)

            def split3_store(c_, r1, dst_fn, dst_t):
                pz = [bs_[0], bs_[1], pts[0]]
                P.op("vector", I("tensor_copy", out=pz[0][0:32, :], in_=c_[0:32, :]), reads=[c_], writes=[pz[0]])
                P.op("vector", I("tensor_tensor", out=r1[0:32, :], in0=c_[0:32, :], in1=pz[0][0:32, :], op=ALU.subtract), reads=[c_, pz[0]], writes=[r1])
                P.op("vector", I("tensor_copy", out=pz[1][0:32, :], in_=r1[0:32, :]), reads=[r1], writes=[pz[1]])
                P.op("vector", I("tensor_tensor", out=r1[0:32, :], in0=r1[0:32, :], in1=pz[1][0:32, :], op=ALU.subtract), reads=[r1, pz[1]], writes=[r1])
                P.op("vector", I("tensor_copy", out=pz[2][0:32, :], in_=r1[0:32, :]), reads=[r1], writes=[pz[2]])
                for k in range(3):
                    P.dma(sq, I("dma_start", out=dst_fn(k), in_=pz[k][0:32, :]), reads=[pz[k]], writes=[dst_t])

            for r in range(8):
                c_ = fs[0]; r1 = fs[1]
                P.dma(sq, I("dma_start", out=c_[0:32, :], in_=w1v[:, r, :]), reads=[w1_all], writes=[c_])
                for s in range(8):
                    P.op("vector", I("tensor_scalar", out=c_[0:32, s * 128:(s + 1) * 128], in0=c_[0:32, s * 128:(s + 1) * 128], scalar1=offs[:, s, r:r + 1], scalar2=-1.0,
                                     op0=ALU.add, op1=ALU.mult), reads=[c_, sm], writes=[c_])
                split3_store(c_, r1, lambda k, r=r: kaug1[k, :, r, :], kaug1)
            P.op("vector", I("tensor_copy", out=sm[0:32, B1:B1 + 64], in_=sm[0:32, B3:B3 + 64]), reads=[sm], writes=[sm])
            for s in range(8):
                P.op("vector", I("tensor_tensor", out=sm[0:32, B1 + s * 8:B1 + s * 8 + 8], in0=sm[0:32, B1 + s * 8:B1 + s * 8 + 8], in1=sm[0:32, C_SEL:C_SEL + 8], op=ALU.mult), reads=[sm], writes=[sm])
                P.op("vector", I("tensor_reduce", out=sm[0:32, B4 + s:B4 + s + 1], in_=sm[0:32, B1 + s * 8:B1 + s * 8 + 8], axis=mybir.AxisListType.X, op=ALU.add), reads=[sm], writes=[sm])
            c_ = fs[0]; r1 = fs[1]
            P.dma(sq, I("dma_start", out=c_[0:32, :], in_=w1_loc[:, :]), reads=[w1_loc], writes=[c_])
            for s in range(8):
                P.op("vector", I("tensor_scalar", out=c_[0:32, s * 128:(s + 1) * 128], in0=c_[0:32, s * 128:(s + 1) * 128], scalar1=sm[0:32, B4 + s:B4 + s + 1], scalar2=None,
                                 op0=ALU.add), reads=[c_, sm], writes=[c_])
            split3_store(c_, r1, lambda k: qaug1[k, :, :], qaug1)

            kt1v = kt1_all.h.rearrange("(r m) t -> m r t", r=8)
            v1v = v1_all.h.rearrange("(r h p) (s d) -> h p r s d", r=8, h=32, d=64)
            set_v_ones()
            KF = 70
            for hd in range(32):
                qt = qts[hd % 2]
                P.dma(sq, I("dma_start", out=qt[0:64, 0, :], in_=qT1[hd * 64:(hd + 1) * 64, :]), reads=[qT1], writes=[qt])
                P.dma(sq, I("dma_start", out=qt[64:67, 0, :], in_=qaug1[:, hd, :]), reads=[qaug1], writes=[qt])
                P.dma(sq, I("dma_start", out=qt[67:70, 0, :], in_=ones3_in[:, 0, :]), writes=[qt])
                gt = gts[hd % 2]
                P.dma(sq, I("dma_start", out=gt[64:128, :], in_=sg1[hd * 64:(hd + 1) * 64, :]), reads=[sg1], writes=[gt])

                def kvl(qc, kt, vt, hd=hd):
                    P.dma(sq, I("dma_start", out=kt[0:64, :].rearrange("m (r t) -> m r t", t=256), in_=kt1v[hd * 64:(hd + 1) * 64, :, qc * 256:(qc + 1) * 256]), reads=[kt1_all], writes=[kt])
                    P.dma(sq, I("dma_start", out=kt[64:67, :].rearrange("m (r t) -> m r t", t=256), in_=ones3_in[:, :, qc * 256:(qc + 1) * 256]), writes=[kt])
                    P.dma(sq, I("dma_start", out=kt[67:70, :].rearrange("m (r t) -> m r t", t=256), in_=kaug1[:, hd, :, qc * 256:(qc + 1) * 256]), reads=[kaug1], writes=[kt])
                    for r in range(8):
                        P.dma(sq, I("dma_start", out=vt[:, 2 * r:2 * r + 2, 0:64], in_=v1v[hd, :, r, 2 * qc:2 * qc + 2, :]), reads=[v1_all], writes=[vt])

                O = PS[2 + (hd % 2)]
                attn_full(KF, qt, None, kvl, "f", O, None)
                rc = fs[hd % 2]
                P.op("vector", I("reciprocal", out=rc[64:128, :], in_=O[64:128, :]), reads=[O], writes=[rc])
                P.op("vector", I("tensor_tensor", out=rc[64:128, :], in0=rc[64:128, :], in1=gt[64:128, :], op=ALU.mult), reads=[rc, gt], writes=[rc])
                hb = (hd % 2) * 64
                P.op("vector", I("tensor_tensor", out=yT[hb:hb + 64, hd // 2, :], in0=O[0:64, :], in1=rc[64:128, :], op=ALU.mult), reads=[O, rc], writes=[yT])

            phase_R(1, w_out_c)

    if stage in (0, 3):
        r = fs[3]
        rstd_bc(r)
        outv = out_t.h.rearrange("(s p) f -> p s f", p=128)
        toks = []
        for kc in range(KC):
            o = fs[kc % 2]
            P.op("vector", I("scalar_tensor_tensor", out=o[:, :], in0=xT[:, kc, :], scalar=gv[:, 4, kc:kc + 1], in1=r[:, :], op0=ALU.mult, op1=ALU.mult),
                 reads=[xT, gv, r], writes=[o])
            pst = PS[kc % 4]
            for s in range(8):
                P.op("tensor", I("transpose", out=pst[:, s * 128:(s + 1) * 128], in_=o[:, s * 128:(s + 1) * 128], identity=ident[:, :]),
                     reads=[o, ident], writes=[pst], pe_accum=True)
            ot = fs[2]
            P.op("scalar", I("activation", out=ot[:, :], in_=pst[:, :], func=AF.Copy), reads=[pst], writes=[ot])
            toks.append(P.dma(sq, I("dma_start", out=outv[:, :, kc * 128:(kc + 1) * 128], in_=ot[:, :].rearrange("p (s f) -> p s f", f=128)), reads=[ot], writes=[out_t]))
        P.wait_all(sq, toks)
    if ext_out:
        P.wait_all(sq, [tk for t in ext_out for tk in t.ws.items()])
    P.emit()
    st.close()
    return nc, in_names, [t.name for t in ext_out]


def _bf(a):
    return np.ascontiguousarray(np.asarray(a, dtype=np.float32)).astype(ml_dtypes.bfloat16)


def _split3(v):
    v = np.asarray(v, dtype=np.float64)
    p1 = v.astype(np.float32).astype(ml_dtypes.bfloat16)
    r = v - p1.astype(np.float64)
    p2 = r.astype(np.float32).astype(ml_dtypes.bfloat16)
    r = r - p2.astype(np.float64)
    p3 = r.astype(np.float32).astype(ml_dtypes.bfloat16)
    return p1, p2, p3


def _const_tables(c):
    j = np.arange(128)[:, None]
    i = np.arange(128)[None, :]
    caus = np.where(j <= i, BIG, 0.0).astype(np.float32)
    prev = np.where(j > i, BIG, 0.0).astype(np.float32)
    maskB = np.zeros((128, 8, 128), np.float32)
    for m in range(8):
        maskB[:, m, :] = BIG if m < c else (caus if m == c else 0.0)
    maskA = np.zeros((128, 9, 128), np.float32)
    for m in range(9):
        if m == c:
            maskA[:, m, :] = prev
        elif m == c + 1:
            maskA[:, m, :] = caus
    s_ = np.arange(8)[:, None]
    t_ = np.arange(128)[None, :]
    blk_q = (8 * s_ + c) * np.ones((1, 128))
    loc = (t_ * np.ones((8, 1)))
    blk_q = blk_q.reshape(-1); loc_q = loc.reshape(-1)
    r_ = np.arange(8)[:, None, None]
    blk_k = (8 * s_[None] + r_) * np.ones((1, 1, 128))
    blk_k = blk_k.reshape(8, 1024); loc_k = np.tile(loc.reshape(1, -1), (8, 1))
    slA = np.array([2.0 ** (-8.0 * (h + 1) / 16) for h in range(16)], np.float64)
    qaugA = np.zeros((16, 9, 1024), ml_dtypes.bfloat16)
    for h in range(16):
        v = -slA[h] * (128.0 * blk_q + loc_q)
        p = _split3(v)
        sp = _split3(np.array([slA[h]]))
        for k in range(3):
            qaugA[h, k] = p[k]
            qaugA[h, 3 + k] = sp[k][0]
            qaugA[h, 6 + k] = sp[k][0]
    kaugA = np.zeros((9, 8, 1024), ml_dtypes.bfloat16)
    kaugA[0:3] = 1.0
    kaugA[3:6] = (128.0 * blk_k)[None].astype(np.float32)
    kaugA[6:9] = loc_k[None].astype(np.float32)
    slB = np.array([2.0 ** (-8.0 * (h + 1) / 8) for h in range(8)], np.float64)
    qaugB = np.zeros((8, 4, 1024), ml_dtypes.bfloat16)
    kaugB = np.zeros((8, 4, 8, 1024), ml_dtypes.bfloat16)
    for h in range(8):
        qaugB[h, 0] = (-slB[h] * 128.0 * blk_q).astype(np.float32)
        qaugB[h, 1] = (-slB[h] * loc_q).astype(np.float32)
        qaugB[h, 2] = 1.0
        qaugB[h, 3] = 1.0
        kaugB[h, 0] = 1.0
        kaugB[h, 1] = 1.0
        kaugB[h, 2] = (slB[h] * 128.0 * blk_k).astype(np.float32)
        kaugB[h, 3] = (slB[h] * loc_k).astype(np.float32)
    sel = np.zeros((128, 8), np.float32); sel[:, c] = 1.0
    rstm = np.ones((32, 1024), np.float32); rstm[:, 0::128] = 0.0
    return dict(maskB=_bf(maskB), maskA=_bf(maskA), qaugA=qaugA, kaugA=kaugA, qaugB=qaugB, kaugB=kaugB, sel=sel,
                ones3=np.ones((3, 8, 1024), ml_dtypes.bfloat16), rstm=rstm, ident=np.eye(128, dtype=np.float32))


_NC_CACHE = {}
FUSED = False


def _get(stage):
    if stage not in _NC_CACHE:
        _NC_CACHE[stage] = build(stage)
    return _NC_CACHE[stage]


def _launch(stage, maps):
    nc, names, outs = _get(stage)
    in_maps = [{k: m[k] for k in names} for m in maps]
    res = run_bass_kernel_spmd(nc, in_maps, core_ids=list(range(NCORES)))
    return res.results


def kernel(x, p, norm_g, w_in_ab, w_out_ab, attn_sinks, diff_lambda, diff_subln_g,
           w_in_c, w_out_c, forget_bias, ple_proj, ple_gate, ple_norm_g, final_norm_g):
    f32 = lambda a: np.ascontiguousarray(np.asarray(a, dtype=np.float32))
    x = f32(x)[0]; p = f32(p)[:, 0]
    xb = x.reshape(8, 8, 128, D)
    pb = p.reshape(2, 8, 8, 128, 256)
    gvec = np.stack([f32(norm_g)[0], f32(norm_g)[1], f32(ple_norm_g)[0], f32(ple_norm_g)[1], f32(final_norm_g)], 0)
    gvec = np.ascontiguousarray(gvec.reshape(5, KC, 128).transpose(0, 2, 1))
    shared = dict(
        w_in_ab=f32(w_in_ab)[0], w_out_ab=f32(w_out_ab)[0], w_in_c=f32(w_in_c)[0], w_out_c=f32(w_out_c)[0],
        ple_proj=f32(ple_proj), ple_gate=f32(ple_gate), gvec=gvec,
        sinks=np.ascontiguousarray(np.tile(f32(attn_sinks).reshape(1, 16), (128, 1))), lam=np.ascontiguousarray(np.tile(f32(diff_lambda).reshape(1, 256), (128, 1))),
        subg=f32(diff_subln_g).reshape(128, 1), fbias=f32(forget_bias).reshape(32, 1))
    maps = []
    for c in range(NCORES):
        m = dict(shared)
        m["x"] = np.ascontiguousarray(xb[:, c].reshape(NT, D))
        m["p"] = np.ascontiguousarray(pb[:, :, c].reshape(2, NT, 256))
        m.update(_const_tables(c))
        maps.append(m)
    if FUSED:
        res = _launch(0, maps)
    else:
        cat = lambda rs, k: np.ascontiguousarray(np.concatenate([np.asarray(r[k]) for r in rs], axis=0))
        r1 = _launch(1, maps)
        for c in range(NCORES):
            for k in ("qT0", "sg0", "xsav1"):
                maps[c][k] = np.asarray(r1[c][k])
        for k in ("kt0", "v0b", "v0a"):
            g = cat(r1, k + "_loc")
            for c in range(NCORES):
                maps[c][k + "_all"] = g
        r2 = _launch(2, maps)
        for c in range(NCORES):
            for k in ("qT1", "sg1", "xsav2", "w1_loc"):
                maps[c][k] = np.asarray(r2[c][k])
        for k in ("kt1", "v1", "w1"):
            g = cat(r2, k + "_loc")
            for c in range(NCORES):
                maps[c][k + "_all"] = g
        res = _launch(3, maps)
    out = np.zeros((8, 8, 128, D), np.float32)
    for c in range(NCORES):
        out[:, c] = np.asarray(res[c]["out"], dtype=np.float32).reshape(8, 128, D)
    return out.reshape(1, 8192, D)
```
